# Optimizing a Trainium2 kernel written in Bass

```python
import jax
import jax.numpy as jnp
from jax import lax
import numpy as np

D_MODEL = 2048
BATCH = 4
SEQ = 2048
DEPTH = 2

CHUNK = 64
D_PL = 256
D_FF = 5632
EPS = 1e-6
N_EVEN = (DEPTH + 1) // 2
N_ODD = DEPTH // 2

GLA_HEADS = 4
GLA_DK = D_MODEL // 4
GLA_DV = D_MODEL // 2
GLA_HEAD_K = GLA_DK // GLA_HEADS
GLA_HEAD_V = GLA_DV // GLA_HEADS
GLA_GATE_RANK = 16
GLA_GATE_TAU = 16.0
CONV_CH = D_MODEL // 2
CONV_WIDTH = 31
AB_SPLITS = [GLA_DK, GLA_DK, GLA_DV, GLA_DV, GLA_GATE_RANK, CONV_CH, CONV_CH]
AB_IN = sum(AB_SPLITS)
AB_OUT = GLA_DV + CONV_CH
ATT_HEADS = 16
ATT_HEAD_DIM = D_MODEL // ATT_HEADS
LEFT_CHUNKS = 8
BAND = (LEFT_CHUNKS + 1) * CHUNK
REL_CLIP = 128

kernel_name = "hybrid_gla_conv_chunkattn_macaron"


def rms_norm(x, g):
    xf = x.astype(jnp.float32)
    y = xf * lax.rsqrt(jnp.mean(xf * xf, axis=-1, keepdims=True) + EPS)
    return (y * g.astype(jnp.float32)).astype(x.dtype)


def layer_norm(x, g, b):
    xf = x.astype(jnp.float32)
    mu = jnp.mean(xf, axis=-1, keepdims=True)
    var = jnp.mean(jnp.square(xf - mu), axis=-1, keepdims=True)
    y = (xf - mu) * lax.rsqrt(var + EPS) * g.astype(jnp.float32) + b.astype(jnp.float32)
    return y.astype(x.dtype)


def swiglu_ffn(h, w_gate, w_up, w_down):
    return (jax.nn.silu(h @ w_gate) * (h @ w_up)) @ w_down


def gla_chunked(q, k, v, log_a):
    B, T, H, dk = q.shape
    dv = v.shape[-1]
    n = T // CHUNK

    def to_chunks(a):
        return a.astype(jnp.float32).reshape(B, n, CHUNK, H, a.shape[-1]).transpose(1, 0, 3, 2, 4)

    qc, kc, vc, gc = to_chunks(q), to_chunks(k), to_chunks(v), to_chunks(log_a)
    causal = jnp.tril(jnp.ones((CHUNK, CHUNK), dtype=bool))

    def step(S, inp):
        qj, kj, vj, gj = inp
        b = jnp.cumsum(gj, axis=2)
        diff = b[:, :, :, None, :] - b[:, :, None, :, :]
        decay = jnp.exp(jnp.where(causal[:, :, None], diff, -jnp.inf))
        A = jnp.einsum('bhtd,bhsd,bhtsd->bhts', qj, kj, decay)
        o = (jnp.einsum('bhtd,bhdv->bhtv', qj * jnp.exp(b), S)
             + jnp.einsum('bhts,bhsv->bhtv', A, vj))
        b_last = b[:, :, -1:, :]
        S = (S * jnp.exp(b_last[:, :, 0, :])[..., None]
             + jnp.einsum('bhsd,bhsv->bhdv', kj * jnp.exp(b_last - b), vj))
        return S, o

    S0 = jnp.zeros((B, H, dk, dv), jnp.float32)
    _, o = lax.scan(step, S0, (qc, kc, vc, gc))
    return o.transpose(1, 0, 3, 2, 4).reshape(B, T, H, dv)


def mixer_gla_conv(h, w_in, gla_gate_w, gla_gate_b, gla_norm_g,
                   conv_dw, conv_dw_b, conv_ln_g, conv_ln_b, w_out):
    B, T, _ = h.shape
    z = h @ w_in
    idx = np.cumsum(AB_SPLITS)[:-1].tolist()
    q, k, v, r, gz, ca, cb = jnp.split(z, idx, axis=-1)

    log_a = jax.nn.log_sigmoid((gz @ gla_gate_w + gla_gate_b).astype(jnp.float32)) / GLA_GATE_TAU
    q = q.reshape(B, T, GLA_HEADS, GLA_HEAD_K) * (GLA_HEAD_K ** -0.5)
    k = k.reshape(B, T, GLA_HEADS, GLA_HEAD_K)
    v = v.reshape(B, T, GLA_HEADS, GLA_HEAD_V)
    log_a = log_a.reshape(B, T, GLA_HEADS, GLA_HEAD_K)
    o = gla_chunked(q, k, v, log_a)
    o = rms_norm(o, gla_norm_g).reshape(B, T, GLA_DV)
    a_out = (o * jax.nn.silu(r.astype(jnp.float32))).astype(h.dtype)

    u = ca * jax.nn.sigmoid(cb)
    rhs = conv_dw.astype(u.dtype).reshape(CONV_WIDTH, 1, CONV_CH)
    u = lax.conv_general_dilated(u, rhs, window_strides=(1,),
                                 padding=[(CONV_WIDTH - 1, 0)],
                                 dimension_numbers=('NWC', 'WIO', 'NWC'),
                                 feature_group_count=CONV_CH)
    u = u + conv_dw_b
    b_out = jax.nn.silu(layer_norm(u, conv_ln_g, conv_ln_b))

    return jnp.concatenate([a_out, b_out], axis=-1) @ w_out


def mixer_chunk_attention(h, w_qkv, rel_bias, w_o):
    B, T, D = h.shape
    n = T // CHUNK
    pad = LEFT_CHUNKS * CHUNK
    q, k, v = jnp.split(h @ w_qkv, 3, axis=-1)
    q = q.reshape(B, n, CHUNK, ATT_HEADS, ATT_HEAD_DIM) * (ATT_HEAD_DIM ** -0.5)
    kp = jnp.pad(k.reshape(B, T, ATT_HEADS, ATT_HEAD_DIM), ((0, 0), (pad, 0), (0, 0), (0, 0)))
    vp = jnp.pad(v.reshape(B, T, ATT_HEADS, ATT_HEAD_DIM), ((0, 0), (pad, 0), (0, 0), (0, 0)))

    t_pos = jnp.arange(CHUNK)
    s_pos = jnp.arange(BAND)
    rel = t_pos[:, None] - s_pos[None, :] + pad
    bias = rel_bias[:, jnp.clip(rel, -REL_CLIP, REL_CLIP) + REL_CLIP].astype(jnp.float32)

    def one_chunk(j):
        qj = lax.dynamic_index_in_dim(q, j, axis=1, keepdims=False)
        kj = lax.dynamic_slice_in_dim(kp, j * CHUNK, BAND, axis=1)
        vj = lax.dynamic_slice_in_dim(vp, j * CHUNK, BAND, axis=1)
        sc = jnp.einsum('bthd,bshd->bhts', qj, kj).astype(jnp.float32) + bias
        valid = (j * CHUNK - pad + s_pos) >= 0
        sc = jnp.where(valid[None, None, None, :], sc, -jnp.inf)
        pr = jax.nn.softmax(sc, axis=-1).astype(vj.dtype)
        return jnp.einsum('bhts,bshd->bthd', pr, vj)

    o = lax.map(one_chunk, jnp.arange(n))
    o = o.transpose(1, 0, 2, 3, 4).reshape(B, T, D)
    return o @ w_o


def setup_inputs(seed: int = 0) -> dict:
    key = jax.random.key(seed)
    ks = jax.random.split(key, 24)
    f32 = jnp.float32

    def w(k, shape, fan_in):
        return jax.random.normal(k, shape, f32) * (fan_in ** -0.5)

    def gain(k, shape):
        return 1.0 + 0.05 * jax.random.normal(k, shape, f32)

    def small(k, shape, scale=0.01):
        return scale * jax.random.normal(k, shape, f32)

    return {
        "x": jax.random.normal(ks[0], (BATCH, SEQ, D_MODEL), f32),
        "p": jax.random.normal(ks[1], (DEPTH, BATCH, SEQ, D_PL), f32),
        "ffn_norm": gain(ks[2], (DEPTH, 2, D_MODEL)),
        "ffn_w_gate": w(ks[3], (DEPTH, 2, D_MODEL, D_FF), D_MODEL),
        "ffn_w_up": w(ks[4], (DEPTH, 2, D_MODEL, D_FF), D_MODEL),
        "ffn_w_down": w(ks[5], (DEPTH, 2, D_FF, D_MODEL), D_FF),
        "mix_norm": gain(ks[6], (DEPTH, D_MODEL)),
        "ab_w_in": w(ks[7], (N_EVEN, D_MODEL, AB_IN), D_MODEL),
        "gla_gate_w": w(ks[8], (N_EVEN, GLA_GATE_RANK, GLA_DK), GLA_GATE_RANK),
        "gla_gate_b": small(ks[9], (N_EVEN, GLA_DK), 0.1),
        "gla_norm_g": gain(ks[10], (N_EVEN, GLA_HEAD_V)),
        "conv_dw": w(ks[11], (N_EVEN, CONV_WIDTH, CONV_CH), CONV_WIDTH),
        "conv_dw_b": small(ks[12], (N_EVEN, CONV_CH)),
        "conv_ln_g": gain(ks[13], (N_EVEN, CONV_CH)),
        "conv_ln_b": small(ks[14], (N_EVEN, CONV_CH)),
        "ab_w_out": w(ks[15], (N_EVEN, AB_OUT, D_MODEL), AB_OUT),
        "att_w_qkv": w(ks[16], (N_ODD, D_MODEL, 3 * D_MODEL), D_MODEL),
        "att_rel_bias": small(ks[17], (N_ODD, ATT_HEADS, 2 * REL_CLIP + 1), 0.1),
        "att_w_o": w(ks[18], (N_ODD, D_MODEL, D_MODEL), D_MODEL),
        "pl_norm": gain(ks[19], (DEPTH, D_MODEL)),
        "pl_w_gate": w(ks[20], (DEPTH, D_MODEL, D_MODEL), D_MODEL),
        "pl_w_proj": w(ks[21], (DEPTH, D_PL, D_MODEL), D_PL),
        "final_norm": gain(ks[22], (D_MODEL,)),
    }


def reference(x, p, ffn_norm, ffn_w_gate, ffn_w_up, ffn_w_down, mix_norm,
              ab_w_in, gla_gate_w, gla_gate_b, gla_norm_g,
              conv_dw, conv_dw_b, conv_ln_g, conv_ln_b, ab_w_out,
              att_w_qkv, att_rel_bias, att_w_o,
              pl_norm, pl_w_gate, pl_w_proj, final_norm):
    for i in range(DEPTH):
        x = x + 0.5 * swiglu_ffn(rms_norm(x, ffn_norm[i, 0]),
                                 ffn_w_gate[i, 0], ffn_w_up[i, 0], ffn_w_down[i, 0])
        h = rms_norm(x, mix_norm[i])
        e = i // 2
        if i % 2 == 0:
            x = x + mixer_gla_conv(h, ab_w_in[e], gla_gate_w[e], gla_gate_b[e], gla_norm_g[e],
                                   conv_dw[e], conv_dw_b[e], conv_ln_g[e], conv_ln_b[e], ab_w_out[e])
        else:
            x = x + mixer_chunk_attention(h, att_w_qkv[e], att_rel_bias[e], att_w_o[e])
        x = x + 0.5 * swiglu_ffn(rms_norm(x, ffn_norm[i, 1]),
                                 ffn_w_gate[i, 1], ffn_w_up[i, 1], ffn_w_down[i, 1])
        gate = jax.nn.sigmoid(rms_norm(x, pl_norm[i]) @ pl_w_gate[i])
        x = x + gate * (p[i] @ pl_w_proj[i])
    return rms_norm(x, final_norm)
```

```python
import os
import numpy as np
import concourse.bass as bass
import concourse.mybir as mybir
from concourse.bass_utils import run_bass_kernel_spmd

F32 = mybir.dt.float32
BF16 = mybir.dt.bfloat16
AF = mybir.ActivationFunctionType
ALU = mybir.AluOpType

NCORES = 8
TOK = 1024
D = 2048
KC = 16
FF = 5632
NGRP = 11
EPS = 1e-6
ABIN = 5136
NEG = -30000.0

R_FFN = 0
R_MIX = 64
R_PL = 96
R_FIN = 128
R_GLN = 144
R_CB = 146
R_LNG = 154
R_LNB = 162
R_DW = 170
NVROW = 512

MODE = "fused"
USE_POOL = False


class _Src:
    def __init__(self, name, sem):
        self.name = name
        self.sem = sem
        self.count = 0


class Prog:
    ENG = ("pe", "act", "dve", "pool", "sp")

    def __init__(self, nc):
        self.nc = nc
        self.src = {n: _Src(n, nc.alloc_semaphore("e_" + n)) for n in self.ENG}
        self.ops = {n: [] for n in self.ENG}
        self.seen = {n: {} for n in self.ENG}
        self.bufs = {}
        self.dss = []
        self.active = True
        self.bank_i = 0
        self.nobar = set()
        self.keep_keys = set()

    def new_ds(self, name):
        s = _Src(name, self.nc.alloc_semaphore("d%d_%s" % (len(self.dss), name)))
        self.dss.append(s)
        return s

    def _wait(self, eng, s, v):
        seen = self.seen[eng]
        if seen.get(s, 0) >= v:
            return
        seen[s] = v
        self.ops[eng].append(lambda e, sem=s.sem, val=v: e.wait_ge(sem, val))

    def _wait_deps(self, eng, reads, writes):
        deps = {}
        me = self.src[eng]
        for k in reads:
            b = self.bufs.get(k)
            if b and b[0]:
                s, v = b[0]
                if v > deps.get(s, 0):
                    deps[s] = v
        for k in writes:
            b = self.bufs.get(k)
            if b:
                if b[0]:
                    s, v = b[0]
                    if v > deps.get(s, 0):
                        deps[s] = v
                for s, v in b[1].items():
                    if s is me:
                        continue
                    if v > deps.get(s, 0):
                        deps[s] = v
        for s, v in deps.items():
            if s is me and eng == "pe":
                continue
            self._wait(eng, s, v)

    def _record(self, src, val, reads, writes):
        for k in reads:
            b = self.bufs.setdefault(k, [None, {}])
            b[1][src] = val
        for k in writes:
            self.bufs[k] = [(src, val), {}]

    def op(self, eng, fn, reads=(), writes=()):
        if not self.active:
            return
        self._wait_deps(eng, reads, writes)
        s = self.src[eng]
        s.count += 1
        self.ops[eng].append(lambda e, f=fn, sem=s.sem: f(e).then_inc(sem, 1))
        self._record(s, s.count, reads, writes)

    def mm_group(self, mms, reads, writes):
        if not self.active:
            return
        self._wait_deps("pe", reads, writes)
        for f in mms[:-1]:
            self.ops["pe"].append(lambda e, f=f: f(e))
        s = self.src["pe"]
        s.count += 1
        self.ops["pe"].append(lambda e, f=mms[-1], sem=s.sem: f(e).then_inc(sem, 1))
        self._record(s, s.count, reads, writes)

    def dma(self, eng, out, in_, reads=(), writes=(), ds=None):
        if not self.active:
            return
        self._wait_deps(eng, reads, writes)
        ds.count += 16
        self.ops[eng].append(lambda e, o=out, i=in_, sem=ds.sem: e.dma_start(out=o, in_=i).then_inc(sem, 16))
        self._record(ds, ds.count, reads, writes)

    def barrier(self):
        if not self.active:
            return
        srcs = [x for x in list(self.src.values()) + self.dss if x not in self.nobar]
        for eng in self.ENG:
            for s in srcs:
                if s.count > 0 and not (s is self.src[eng]):
                    self._wait(eng, s, s.count)
                elif s.count > 0 and eng not in ("pe", "sp", "pool"):
                    self._wait(eng, s, s.count)
        self.bufs = {k: v for k, v in self.bufs.items() if k in self.keep_keys}

    def wait_all(self, eng):
        for s in list(self.src.values()) + self.dss:
            if s.count > 0 and s is not self.src[eng]:
                self._wait(eng, s, s.count)

    def bank(self):
        b = self.bank_i % 8
        self.bank_i += 1
        return b

    def emit(self):
        nc = self.nc
        with nc.Block() as block:
            @block.tensor
            def _(e):
                for f in self.ops["pe"]:
                    f(e)

            @block.scalar
            def _(e):
                for f in self.ops["act"]:
                    f(e)

            @block.vector
            def _(e):
                for f in self.ops["dve"]:
                    f(e)

            @block.gpsimd
            def _(e):
                for f in self.ops["pool"]:
                    f(e)

            @block.sync
            def _(e):
                for f in self.ops["sp"]:
                    f(e)


class Builder:
    def __init__(self, seg, stop=None):
        self.seg = seg
        self.stop = stop
        nc = bass.Bass("TRN2", target_bir_lowering=False)
        self.nc = nc
        self.P = Prog(nc)
        self.dram = {}
        self.in_names = []
        self.out_names = []
        self.XT = nc.alloc_sbuf_tensor("XT", [128, KC, TOK], F32).ap()
        self.HT = nc.alloc_sbuf_tensor("HT", [128, KC, TOK], BF16).ap()
        self.CF = nc.alloc_sbuf_tensor("CF", [128, 512], F32).ap()
        self.VEC = nc.alloc_sbuf_tensor("VEC", [128, NVROW], F32).ap()
        self.ONESB = nc.alloc_sbuf_tensor("ONESB", [128, 128], BF16).ap()
        self.IDENTB = nc.alloc_sbuf_tensor("IDENTB", [128, 128], BF16).ap()
        self.FL = nc.alloc_sbuf_tensor("FL", [128, 4], F32).ap()
        rem = nc.sbuf_bytes_remaining
        self.ARB = (min(rem, 109 * 1024) // 1024) * 1024 - 1024
        self.AR = nc.alloc_sbuf_tensor("AR", [128, self.ARB // 2], BF16).ap()
        self.PS = nc.alloc_psum_tensor("PS", [128, 8, 512], F32).ap()
        self.o_sq = self.ARB - 10 * 1024
        self.o_rstd = self.ARB - 2 * 1024
        self.GEN_END = self.o_sq
        self.ring_ds = [self.P.new_ds("ring%d" % i) for i in range(5)]
        self.misc_ds = {}
        self.cur_seg = 0
        self._set_active()

    def _set_active(self):
        self.P.active = (self.seg is None or self.seg == self.cur_seg)

    def ds(self, name):
        if name not in self.misc_ds:
            self.misc_ds[name] = self.P.new_ds(name)
        return self.misc_ds[name]

    def din(self, name, shape, dtype=F32):
        if name not in self.dram:
            self.dram[name] = self.nc.dram_tensor(name, list(shape), dtype, kind="ExternalInput").ap()
            self.in_names.append(name)
        return self.dram[name]

    def dout(self, name, shape, dtype=F32):
        if name not in self.dram:
            self.dram[name] = self.nc.dram_tensor(name, list(shape), dtype, kind="ExternalOutput").ap()
            self.out_names.append(name)
        return self.dram[name]

    def dscratch(self, name, shape, dtype):
        if name not in self.dram:
            self.dram[name] = self.nc.dram_tensor(name, list(shape), dtype, kind="Internal").ap()
        return self.dram[name]

    def av(self, off, shape, dtype):
        esz = 4 if dtype == F32 else 2
        n = 1
        for s in shape[1:]:
            n *= s
        nb = n * esz
        assert off % 4 == 0 and off + nb <= self.ARB, (off, nb, self.ARB)
        v = self.AR[0:shape[0], off // 2:(off + nb) // 2]
        if dtype == F32:
            v = v.bitcast(F32)
        if len(shape) == 3:
            v = v.rearrange("p (a b) -> p a b", a=shape[1])
        return v

    def psb(self, b):
        return self.PS[:, b, :]

    def ring_begin(self, nslots, tiles):
        self.r_n = nslots
        self.r_tiles = tiles
        self.r_issued = 0
        self.r_next = 0
        self.r_done = 0
        self.r_views = {}
        self._ring_pump()

    def _ring_pump(self):
        lim = min(len(self.r_tiles), self.r_done + self.r_n)
        while self.r_issued < lim:
            self._ring_issue(self.r_issued)
            self.r_issued += 1

    def ring_done(self, k=1):
        self.r_done += k
        self._ring_pump()

    def _ring_issue(self, i):
        P = self.P
        s = i % self.r_n
        pieces = self.r_tiles[i]
        a = pieces[0][1]
        btot = sum(pc[2] for pc in pieces)
        assert a * btot * 2 <= 16384
        v = self.av(s * 16384, [128, a, btot], BF16)
        c0 = 0
        for (src, a_, b_) in pieces:
            P.dma("pool", out=v[:, :, c0:c0 + b_], in_=src, writes=[("ring", s)], ds=self.ring_ds[s])
            c0 += b_
        self.r_views[i] = (v, ("ring", s))

    def ring_get(self):
        i = self.r_next
        self.r_next += 1
        assert i < self.r_issued, "ring: tile not issued (missing ring_done?)"
        return self.r_views.pop(i)

    def ph_consts(self):
        P = self.P
        if not P.active:
            return
        consts = self.din("consts", [128, 512])
        vecs = self.din("vecs", [NVROW, 128])
        flags = self.din("flags", [128, 4])
        d = self.ds("c0")
        P.dma("sp", out=self.CF, in_=consts, writes=["CF"], ds=d)
        d2 = self.ds("c1")
        P.dma("sp", out=self.FL, in_=flags, writes=["FL"], ds=d2)
        vs = self.av(0, [128, 4, 128], F32)
        d3 = self.ds("c2")
        P.dma("sp", out=vs, in_=vecs.rearrange("(j p) c -> p j c", p=128), writes=["vs"], ds=d3)
        b = P.bank()
        pb = self.psb(b).rearrange("p (a b) -> p a b", a=4)
        ident = self.CF[:, 0:128]
        mms = [(lambda e, j=j: e.transpose(out=pb[:, j, :], in_=vs[:, j, :], identity=ident)) for j in range(4)]
        P.mm_group(mms, reads=["vs", "CF"], writes=[("PS", b)])
        P.op("dve", lambda e: e.tensor_copy(out=self.VEC, in_=self.psb(b)), reads=[("PS", b)], writes=["VEC"])
        P.op("dve", lambda e: e.tensor_copy(out=self.ONESB, in_=self.CF[:, 384:512]), reads=["CF"], writes=["ONESB"])
        P.op("dve", lambda e: e.tensor_copy(out=self.IDENTB, in_=self.CF[:, 0:128]), reads=["CF"], writes=["IDENTB"])

    def ph_load_x(self):
        P = self.P
        if not P.active:
            return
        P.barrier()
        x = self.din("x", [TOK, D])
        ident = self.CF[:, 0:128]
        xs = [self.av(i * 8192, [128, D], F32) for i in range(2)]
        dss = [self.ds("xs0"), self.ds("xs1")]
        for tt in range(8):
            s = tt % 2
            P.dma("sp", out=xs[s], in_=x[tt * 128:(tt + 1) * 128, :], writes=[("xs", s)], ds=dss[s])
            for g in range(4):
                b = P.bank()
                pb = self.psb(b).rearrange("p (a b) -> p a b", a=4)
                mms = [(lambda e, i=i, g=g, s=s, pb=pb: e.transpose(out=pb[:, i, :], in_=xs[s][:, (4 * g + i) * 128:(4 * g + i + 1) * 128], identity=ident))
                       for i in range(4)]
                P.mm_group(mms, reads=[("xs", s), "CF"], writes=[("PS", b)])
                dst = self.XT[:, 4 * g:4 * g + 4, tt * 128:(tt + 1) * 128]
                eng = "dve" if g % 2 == 0 else "act"
                if eng == "dve":
                    P.op("dve", lambda e, dst=dst, pb=pb: e.tensor_copy(out=dst, in_=pb), reads=[("PS", b)], writes=[("XTl", tt, g)])
                else:
                    P.op("act", lambda e, dst=dst, pb=pb: e.activation(out=dst, in_=pb, func=AF.Copy), reads=[("PS", b)], writes=[("XTl", tt, g)])

    def _stats(self, th, src_fn, nchunks, scale, rd_keys):
        P = self.P
        SQ = [self.av(self.o_sq + i * 4096, [128, 4, 512], BF16) for i in range(2)]
        RSTD = self.av(self.o_rstd, [128, 512], F32)
        b = P.bank()
        ps = self.psb(b)
        ngr = (nchunks + 3) // 4
        for q in range(ngr):
            n = min(4, nchunks - 4 * q)
            sq = SQ[q % 2]
            P.op("act", lambda e, q=q, n=n, sq=sq: e.activation(out=sq[:, 0:n, :], in_=src_fn(4 * q, n, th), func=AF.Square),
                 reads=rd_keys, writes=[("SQ", q % 2)])
            mms = [(lambda e, i=i, q=q, n=n, sq=sq: e.matmul(ps, lhsT=self.ONESB, rhs=sq[:, i, :], start=(q == 0 and i == 0), stop=(q == ngr - 1 and i == n - 1)))
                   for i in range(n)]
            P.mm_group(mms, reads=[("SQ", q % 2), "ONESB"], writes=[("PS", b)])
        P.op("act", lambda e: e.activation(out=RSTD, in_=ps, func=AF.Sqrt, bias=EPS, scale=scale), reads=[("PS", b)], writes=["RSTD"])
        P.op("dve", lambda e: e.reciprocal(out=RSTD, in_=RSTD), reads=["RSTD"], writes=["RSTD"])
        return RSTD

    def ph_norm(self, row):
        P = self.P
        if not P.active:
            return
        P.barrier()
        self._norm(row)

    def _norm(self, row):
        P = self.P
        XT, HT, VEC = self.XT, self.HT, self.VEC
        for th in range(2):
            ts = slice(th * 512, (th + 1) * 512)
            rd = [("XT", kc, th) for kc in range(KC)]
            RSTD = self._stats(th, lambda c0, n, th_: XT[:, c0:c0 + n, th_ * 512:(th_ + 1) * 512], KC, 1.0 / D, rd)
            pool_kc = [2, 5, 8, 11, 14] if USE_POOL else []
            dve_kc = [kc for kc in range(KC) if kc not in pool_kc]
            for kc in pool_kc:
                P.op("pool", lambda e, kc=kc, ts=ts: e.scalar_tensor_tensor(out=HT[:, kc, ts], in0=XT[:, kc, ts], scalar=VEC[:, row + kc:row + kc + 1],
                                                                             in1=RSTD, op0=ALU.mult, op1=ALU.mult),
                     reads=[("XT", kc, th), "RSTD", "VEC"], writes=[("HTp", th)])
            for kc in dve_kc:
                extra = [("HTp", th)] if (kc == dve_kc[-1] and pool_kc) else []
                P.op("dve", lambda e, kc=kc, ts=ts: e.scalar_tensor_tensor(out=HT[:, kc, ts], in0=XT[:, kc, ts], scalar=VEC[:, row + kc:row + kc + 1],
                                                                            in1=RSTD, op0=ALU.mult, op1=ALU.mult),
                     reads=[("XT", kc, th), "RSTD", "VEC"] + extra, writes=[("HT", th)])

    def ph_ffn(self, idx, norm_row):
        P = self.P
        if not P.active:
            return
        P.barrier()
        XT, HT = self.XT, self.HT
        Wg = self.din("ffn_w_gate", [4 * D, FF])[idx * D:(idx + 1) * D, :].rearrange("(kc p) n -> p kc n", p=128)
        Wu = self.din("ffn_w_up", [4 * D, FF])[idx * D:(idx + 1) * D, :].rearrange("(kc p) n -> p kc n", p=128)
        Wd = self.din("ffn_w_down", [4 * FF, D])[idx * FF:(idx + 1) * FF, :].rearrange("(c p) n -> p c n", p=128)
        tiles = []
        for g in range(NGRP):
            tiles.append([(Wg[:, :, g * 512:(g + 1) * 512], 16, 512)])
            tiles.append([(Wu[:, :, g * 512:(g + 1) * 512], 16, 512)])
            tiles.append([(Wd[:, 4 * g:4 * g + 4, :], 4, 2048)])
        self.ring_begin(5, tiles)
        self._norm(norm_row)
        o = 5 * 16384
        A = [self.av(o + i * 8192, [128, 4, TOK], BF16) for i in range(2)]
        TMP = [self.av(o + 16384 + i * 2048, [128, 512], F32) for i in range(2)]
        ti = 0
        for g in range(NGRP):
            wg, kg = self.ring_get()
            wu, ku = self.ring_get()
            Ab = A[g % 2]
            for c in range(4):
                for th in range(2):
                    ts = slice(th * 512, (th + 1) * 512)
                    bg = P.bank()
                    bu = P.bank()
                    pg, pu = self.psb(bg), self.psb(bu)
                    cs = slice(c * 128, (c + 1) * 128)
                    P.mm_group([(lambda e, kc=kc, pg=pg, wg=wg, cs=cs, ts=ts: e.matmul(pg, lhsT=wg[:, kc, cs], rhs=HT[:, kc, ts], start=(kc == 0), stop=(kc == KC - 1)))
                                for kc in range(KC)], reads=[kg, ("HT", th)], writes=[("PS", bg)])
                    P.mm_group([(lambda e, kc=kc, pu=pu, wu=wu, cs=cs, ts=ts: e.matmul(pu, lhsT=wu[:, kc, cs], rhs=HT[:, kc, ts], start=(kc == 0), stop=(kc == KC - 1)))
                                for kc in range(KC)], reads=[ku, ("HT", th)], writes=[("PS", bu)])
                    tmp = TMP[ti % 2]
                    tk = ("TMP", ti % 2)
                    ti += 1
                    P.op("act", lambda e, tmp=tmp, pg=pg: e.activation(out=tmp, in_=pg, func=AF.Silu), reads=[("PS", bg)], writes=[tk])
                    P.op("dve", lambda e, tmp=tmp, pu=pu, Ab=Ab, c=c, ts=ts: e.tensor_tensor(out=Ab[:, c, ts], in0=tmp, in1=pu, op=ALU.mult),
                         reads=[tk, ("PS", bu)], writes=[("A", g % 2, c, th)])
            self.ring_done(2)
            wd, kd = self.ring_get()
            for dmc in range(KC):
                ds_ = slice(dmc * 128, (dmc + 1) * 128)
                for th in range(2):
                    ts = slice(th * 512, (th + 1) * 512)
                    bo = P.bank()
                    po = self.psb(bo)
                    P.mm_group([(lambda e, c=c, po=po, wd=wd, ds_=ds_, ts=ts, Ab=Ab: e.matmul(po, lhsT=wd[:, c, ds_], rhs=Ab[:, c, ts], start=(c == 0), stop=(c == 3)))
                                for c in range(4)], reads=[kd] + [("A", g % 2, c, th) for c in range(4)], writes=[("PS", bo)])
                    P.op("dve", lambda e, po=po, dmc=dmc, ts=ts: e.scalar_tensor_tensor(out=XT[:, dmc, ts], in0=po, scalar=0.5, in1=XT[:, dmc, ts],
                                                                                          op0=ALU.mult, op1=ALU.add),
                         reads=[("PS", bo), ("XT", dmc, th)], writes=[("XT", dmc, th)])
            self.ring_done(1)

    def ph_pl(self, l, norm_row):
        P = self.P
        if not P.active:
            return
        P.barrier()
        XT, HT = self.XT, self.HT
        ident = self.CF[:, 0:128]
        Wg = self.din("pl_w_gate", [2 * D, D])[l * D:(l + 1) * D, :].rearrange("(kc p) n -> p kc n", p=128)
        Wp = self.din("pl_w_proj", [2 * 256, D])[l * 256:(l + 1) * 256, :].rearrange("(k p) n -> p k n", p=128)
        pin = self.din("p", [2 * TOK, 256])[l * TOK:(l + 1) * TOK, :]
        tiles = [[(Wp, 2, 2048)]] + [[(Wg[:, :, t * 512:(t + 1) * 512], 16, 512)] for t in range(4)]
        self.ring_begin(5, tiles)
        o = 5 * 16384
        PT = self.av(o, [128, 2, TOK], BF16)
        pst = self.av(o + 4096, [128, 8, 256], F32)
        TMP = [self.av(o + 12288 + i * 2048, [128, 512], F32) for i in range(2)]
        TM2 = [self.av(o + 16384 + i * 2048, [128, 512], F32) for i in range(2)]
        P.dma("sp", out=pst, in_=pin.rearrange("(tt p) c -> p tt c", p=128), writes=["pst"], ds=self.ds("pst"))
        for tt in range(8):
            b = P.bank()
            pb = self.psb(b).rearrange("p (a b) -> p a b", a=4)
            P.mm_group([(lambda e, k=k, tt=tt, pb=pb: e.transpose(out=pb[:, k, :], in_=pst[:, tt, k * 128:(k + 1) * 128], identity=ident)) for k in range(2)],
                       reads=["pst", "CF"], writes=[("PS", b)])
            P.op("dve", lambda e, tt=tt, pb=pb: e.tensor_copy(out=PT[:, :, tt * 128:(tt + 1) * 128], in_=pb[:, 0:2, :]), reads=[("PS", b)], writes=[("PT", tt // 4)])
        self._norm(norm_row)
        wp, kp = self.ring_get()
        ti = 0
        for t in range(4):
            w, kw = self.ring_get()
            for cc in range(4):
                dmc = 4 * t + cc
                cs = slice(cc * 128, (cc + 1) * 128)
                ds_ = slice(dmc * 128, (dmc + 1) * 128)
                for th in range(2):
                    ts = slice(th * 512, (th + 1) * 512)
                    bg = P.bank()
                    bp = P.bank()
                    pg, pp = self.psb(bg), self.psb(bp)
                    P.mm_group([(lambda e, kc=kc, pg=pg, w=w, cs=cs, ts=ts: e.matmul(pg, lhsT=w[:, kc, cs], rhs=HT[:, kc, ts], start=(kc == 0), stop=(kc == KC - 1)))
                                for kc in range(KC)], reads=[kw, ("HT", th)], writes=[("PS", bg)])
                    P.mm_group([(lambda e, k=k, pp=pp, ds_=ds_, ts=ts: e.matmul(pp, lhsT=wp[:, k, ds_], rhs=PT[:, k, ts], start=(k == 0), stop=(k == 1)))
                                for k in range(2)], reads=[kp, ("PT", th)], writes=[("PS", bp)])
                    tmp, tm2 = TMP[ti % 2], TM2[ti % 2]
                    k1, k2 = ("TMP", ti % 2), ("TM2", ti % 2)
                    ti += 1
                    P.op("act", lambda e, tmp=tmp, pg=pg: e.activation(out=tmp, in_=pg, func=AF.Sigmoid), reads=[("PS", bg)], writes=[k1])
                    P.op("dve", lambda e, tmp=tmp, tm2=tm2, pp=pp: e.tensor_tensor(out=tm2, in0=tmp, in1=pp, op=ALU.mult), reads=[k1, ("PS", bp)], writes=[k2])
                    P.op("dve", lambda e, tm2=tm2, dmc=dmc, ts=ts: e.tensor_tensor(out=XT[:, dmc, ts], in0=XT[:, dmc, ts], in1=tm2, op=ALU.add),
                         reads=[k2, ("XT", dmc, th)], writes=[("XT", dmc, th)])
            self.ring_done(1)

    def ph_final(self, normalize=True):
        P = self.P
        if not P.active:
            return
        P.barrier()
        XT, VEC = self.XT, self.VEC
        ident = self.CF[:, 0:128]
        out = self.dout("out", [TOK, D])
        if normalize:
            for th in range(2):
                ts = slice(th * 512, (th + 1) * 512)
                rd = [("XT", kc, th) for kc in range(KC)]
                RSTD = self._stats(th, lambda c0, n, th_: XT[:, c0:c0 + n, th_ * 512:(th_ + 1) * 512], KC, 1.0 / D, rd)
                for kc in range(KC):
                    P.op("dve", lambda e, kc=kc, ts=ts: e.scalar_tensor_tensor(out=XT[:, kc, ts], in0=XT[:, kc, ts], scalar=VEC[:, R_FIN + kc:R_FIN + kc + 1],
                                                                                in1=RSTD, op0=ALU.mult, op1=ALU.mult),
                         reads=[("XT", kc, th), "RSTD", "VEC"], writes=[("XT", kc, th)])
        osb = [self.av(i * 8192, [128, D], F32) for i in range(2)]
        dss = [self.ds("os0"), self.ds("os1")]
        for tt in range(8):
            s = tt % 2
            th = tt // 4
            for g in range(4):
                b = P.bank()
                pb = self.psb(b).rearrange("p (a b) -> p a b", a=4)
                P.mm_group([(lambda e, i=i, g=g, tt=tt, pb=pb: e.transpose(out=pb[:, i, :], in_=XT[:, 4 * g + i, tt * 128:(tt + 1) * 128], identity=ident))
                            for i in range(4)], reads=[("XT", 4 * g + i, th) for i in range(4)] + ["CF"], writes=[("PS", b)])
                dst = osb[s][:, g * 512:(g + 1) * 512]
                if g % 2 == 0:
                    P.op("dve", lambda e, dst=dst, b=b: e.tensor_copy(out=dst, in_=self.psb(b)), reads=[("PS", b)], writes=[("os", s, g)])
                else:
                    P.op("act", lambda e, dst=dst, b=b: e.activation(out=dst, in_=self.psb(b), func=AF.Copy), reads=[("PS", b)], writes=[("os", s, g)])
            P.dma("sp", out=out[tt * 128:(tt + 1) * 128, :], in_=osb[s], reads=[("os", s, g) for g in range(4)], writes=[("outd", tt)], ds=dss[s])

    def ph_dbg_ht(self):
        P = self.P
        P.barrier()
        for kc in range(KC):
            for th in range(2):
                ts = slice(th * 512, (th + 1) * 512)
                P.op("dve", lambda e, kc=kc, ts=ts: e.tensor_copy(out=self.XT[:, kc, ts], in_=self.HT[:, kc, ts]),
                     reads=[("HT", th)], writes=[("XT", kc, th)])

    def dbg_dump(self, name, ap, shape, dtype):
        if not os.environ.get("K_DUMP"):
            return
        t = self.dout("dbg_" + name, shape, dtype)
        self.P.wait_all("sp")
        self.P.dma("sp", out=t, in_=ap, writes=[("dbg", name)], ds=self.ds("dbg_" + name))

    def xt(self, name, shape, dtype, producer):
        if self.seg is None:
            return self.dscratch(name, shape, dtype)
        if producer:
            return self.dout(name + "_o", shape, dtype)
        return self.din(name + "_i", shape, dtype)

    def exchange(self, k):
        P = self.P
        if self.seg is None:
            self.fused_exchange(k)
        else:
            if P.active:
                self.dump_state()
        self.cur_seg = k + 1
        self._set_active()
        if self.seg is not None and P.active:
            self.restore_state()

    def _coll(self, name, shape, dt, rkeys, wkey):
        P = self.P
        if not P.active or self.seg is not None:
            return
        groups = [[0, 1], [2, 3], [4, 5], [6, 7]]
        src = self.dram[name]
        dst = self.dscratch(name + "_g", [2 * shape[0], shape[1]], dt)
        cs = self.ds("cc_" + name)
        P.nobar.add(cs)
        P.keep_keys.add(wkey)
        P._wait_deps("pool", rkeys, [wkey])
        cs.count += 1
        P.ops["pool"].append(lambda e, src=src, dst=dst, sem=cs.sem: e.collective_compute(
            "AllGather", ALU.bypass, replica_groups=groups, ins=[src], outs=[dst]).then_inc(sem, 1))
        P._record(cs, cs.count, rkeys, [wkey])

    def fused_exchange(self, k):
        if k == 0:
            self._coll("ex1_u", [128, 240], BF16, ["d_ex1_u"], "d_ex1_ug")

    def halo_src(self, name, shape, dtype):
        if self.seg is None:
            return self.dram[name + "_g"][0:shape[0]]
        return self.din(name + "_h", shape, dtype)

    def _m0_layout(self):
        R0 = 2 * 16384
        L = {}
        L["O"] = self.av(R0, [128, 8, TOK], BF16)
        L["QH"] = self.av(R0 + 16384, [128, 4, TOK], BF16)
        L["SE"] = self.av(R0 + 24576, [128, 4, 256], F32)
        T0 = R0 + 28672
        L["T0"] = T0
        L["U"] = self.av(T0, [128, 8, 1054], BF16)
        L["SR"] = self.av(T0 + 17664, [128, 8, TOK], BF16)
        L["X0"] = T0 + 17664 + 16384
        return L

    def ph_mixer0_a(self, norm_row):
        P = self.P
        if not P.active:
            return
        P.barrier()
        HT, CF = self.HT, self.CF
        MASK = CF[:, 128:256]
        LM = CF[:, 256:384]
        Win = self.din("ab_w_in", [D, ABIN]).rearrange("(kc p) n -> p kc n", p=128)
        gw_d = self.din("gwaug", [17, 512])
        L = self._m0_layout()
        O, QH, SE = L["O"], L["QH"], L["SE"]
        T0 = L["T0"]
        GZ = self.av(T0, [32, TOK], BF16)
        GW = self.av(T0 + 2048, [32, 512], BF16)
        LA = self.av(T0 + 3072, [128, 8, 128], F32)
        ENB = self.av(T0 + 7168, [128, 8, 128], F32)
        EBT = self.av(T0 + 11264, [128, TOK], F32)
        ENBT = self.av(T0 + 15360, [128, TOK], F32)
        KT = self.av(T0 + 19456, [128, TOK], BF16)
        KTM = self.av(T0 + 21504, [128, 8, 128], BF16)
        VH = self.av(T0 + 23552, [128, 8, 256], BF16)
        S32 = self.av(T0 + 27648, [128, 256], F32)
        SBF = self.av(T0 + 28672, [128, 256], BF16)
        TT = self.av(T0 + 29184, [128, 256], F32)
        AM = [self.av(T0 + 30208 + i * 256, [128, 128], BF16) for i in range(2)]
        EBL = self.av(T0 + 30720, [128, 16], F32)
        EC = self.av(T0 + 30784, [128, 16], F32)
        SBFR = [self.av(T0 + 30848 + i * 512, [128, 256], BF16) for i in range(8)]
        TTR = [TT, self.av(T0 + 34944, [128, 256], F32)]
        tiles = [[(Win[:, :, 3072:3088], 16, 16)]]
        for h in range(4):
            tiles.append([(Win[:, :, h * 128:(h + 1) * 128], 16, 128),
                          (Win[:, :, 512 + h * 128:512 + (h + 1) * 128], 16, 128),
                          (Win[:, :, 1024 + h * 256:1024 + (h + 1) * 256], 16, 256)])
        tiles.append([(Win[:, :, 2048:2560], 16, 512)])
        tiles.append([(Win[:, :, 2560:3072], 16, 512)])
        for cb in range(2):
            tiles.append([(Win[:, :, 3088 + cb * 512:3088 + (cb + 1) * 512], 16, 512)])
            tiles.append([(Win[:, :, 4112 + cb * 512:4112 + (cb + 1) * 512], 16, 512)])
        self.ring_begin(2, tiles)
        self._norm(norm_row)
        qscale = 128.0 ** -0.5
        P.op("dve", lambda e: e.memset(GZ, 1.0), writes=["GZ"])
        P.op("dve", lambda e: e.memset(GW, 0.0), writes=["GW"])
        P.dma("pool", out=GW[0:17, :], in_=gw_d, reads=[], writes=["GW"], ds=self.ds("gw"))
        wz, kz = self.ring_get()
        for th in range(2):
            ts = slice(th * 512, (th + 1) * 512)
            b = P.bank()
            ps = self.psb(b)
            P.mm_group([(lambda e, kc=kc, ps=ps, ts=ts: e.matmul(ps[0:16, :], lhsT=wz[:, kc, 0:16], rhs=HT[:, kc, ts], start=(kc == 0), stop=(kc == KC - 1)))
                        for kc in range(KC)], reads=[kz, ("HT", th)], writes=[("PS", b)])
            P.op("dve", lambda e, ps=ps, ts=ts: e.tensor_copy(out=GZ[0:16, ts], in_=ps[0:16, :]), reads=[("PS", b), "GZ"], writes=["GZ"])
        self.ring_done(1)
        DBG = int(os.environ.get("K_DBG", "99"))
        if DBG == 0:
            return
        def gla_head(h, w, kw):
            hs = slice(h * 128, (h + 1) * 128)
            for q in range(2):
                b = P.bank()
                pb = self.psb(b).rearrange("p (a b) -> p a b", a=4)
                P.mm_group([(lambda e, i=i, q=q, pb=pb: e.matmul(pb[:, i, :], lhsT=GZ[0:17, (4 * q + i) * 128:(4 * q + i + 1) * 128], rhs=GW[0:17, hs], start=True, stop=True))
                            for i in range(4)], reads=["GZ", "GW"], writes=[("PS", b)])
                P.op("act", lambda e, q=q, pb=pb: e.activation(out=LA[:, 4 * q:4 * q + 4, :], in_=pb, func=AF.Exp, scale=-1.0), reads=[("PS", b)], writes=["LA"])
                P.op("act", lambda e, q=q: e.activation(out=LA[:, 4 * q:4 * q + 4, :], in_=LA[:, 4 * q:4 * q + 4, :], func=AF.Ln, bias=1.0), reads=["LA"], writes=["LA"])
            for q in range(2):
                b = P.bank()
                pb = self.psb(b).rearrange("p (a b) -> p a b", a=4)
                P.mm_group([(lambda e, i=i, q=q, pb=pb: e.matmul(pb[:, i, :], lhsT=LM, rhs=LA[:, 4 * q + i, :], start=True, stop=True)) for i in range(4)],
                           reads=["LA", "CF"], writes=[("PS", b)])
                P.op("act", lambda e, q=q, pb=pb: e.activation(out=ENB[:, 4 * q:4 * q + 4, :], in_=pb, func=AF.Exp, scale=-1.0), reads=[("PS", b)], writes=["ENB"])
                b2 = P.bank()
                pb2 = self.psb(b2).rearrange("p (a b) -> p a b", a=4)
                P.mm_group([(lambda e, i=i, q=q, pb2=pb2: e.matmul(pb2[:, i, :], lhsT=LA[:, 4 * q + i, :], rhs=LM, start=True, stop=True)) for i in range(4)],
                           reads=["LA", "CF"], writes=[("PS", b2)])
                P.op("act", lambda e, q=q, b2=b2: e.activation(out=EBT[:, q * 512:(q + 1) * 512], in_=self.psb(b2), func=AF.Exp), reads=[("PS", b2)], writes=["EBT"])
            if DBG == 1:
                return
            P.op("dve", lambda e: e.reciprocal(out=ENBT, in_=EBT), reads=["EBT"], writes=["ENBT"])
            P.op("dve", lambda e: e.tensor_copy(out=EBL, in_=EBT.rearrange("p (j c) -> p j c", c=64)[:, :, 63]), reads=["EBT"], writes=["EBL"])
            if DBG == 2:
                return
            for th in range(2):
                ts = slice(th * 512, (th + 1) * 512)
                b = P.bank()
                ps = self.psb(b)
                P.mm_group([(lambda e, kc=kc, ps=ps, ts=ts: e.matmul(ps, lhsT=w[:, kc, 0:128], rhs=HT[:, kc, ts], start=(kc == 0), stop=(kc == KC - 1)))
                            for kc in range(KC)], reads=[kw, ("HT", th)], writes=[("PS", b)])
                P.op("dve", lambda e, ps=ps, ts=ts: e.scalar_tensor_tensor(out=QH[:, h, ts], in0=ps, scalar=qscale, in1=EBT[:, ts], op0=ALU.mult, op1=ALU.mult),
                     reads=[("PS", b), "EBT"], writes=[("QH", h)])
                b = P.bank()
                ps = self.psb(b)
                P.mm_group([(lambda e, kc=kc, ps=ps, ts=ts: e.matmul(ps, lhsT=w[:, kc, 128:256], rhs=HT[:, kc, ts], start=(kc == 0), stop=(kc == KC - 1)))
                            for kc in range(KC)], reads=[kw, ("HT", th)], writes=[("PS", b)])
                P.op("dve", lambda e, ps=ps, ts=ts: e.tensor_tensor(out=KT[:, ts], in0=ps, in1=ENBT[:, ts], op=ALU.mult), reads=[("PS", b), "ENBT"], writes=["KT"])
            for tt in range(8):
                tsl = slice(tt * 128, (tt + 1) * 128)
                b = P.bank()
                ps = self.psb(b)
                P.mm_group([(lambda e, kc=kc, ps=ps, tsl=tsl: e.matmul(ps[:, 0:384], lhsT=HT[:, kc, tsl], rhs=w[:, kc, 128:512], start=(kc == 0), stop=(kc == KC - 1)))
                            for kc in range(KC)], reads=[kw, ("HT", tt // 4)], writes=[("PS", b)])
                P.op("dve", lambda e, ps=ps, tt=tt: e.tensor_tensor(out=KTM[:, tt, :], in0=ps[:, 0:128], in1=ENB[:, tt, :], op=ALU.mult), reads=[("PS", b), "ENB"], writes=["KTM", ("PSR", b)])
                P.op("act", lambda e, ps=ps, tt=tt: e.activation(out=VH[:, tt, :], in_=ps[:, 128:384], func=AF.Copy), reads=[("PS", b), ("PSR", b)], writes=["VH"])
            self.ring_done(1)
            if DBG == 3:
                return
            P.op("dve", lambda e: e.memset(S32, 0.0), writes=["S32"])

            def finish_tile(tt, po, bo):
                tsl = slice(tt * 128, (tt + 1) * 128)
                for hh in range(2):
                    j = 2 * tt + hh
                    if j == 0:
                        continue
                    sb = SBFR[(j - 1) % 8]
                    sk = ("SBF", (j - 1) % 8)
                    cs = slice(j * 64, (j + 1) * 64)
                    P.mm_group([(lambda e, vc=vc, hh=hh, cs=cs, sb=sb: e.matmul(po[:, vc, hh * 64:(hh + 1) * 64], lhsT=sb[:, vc * 128:(vc + 1) * 128], rhs=QH[:, h, cs],
                                                                                   start=False, stop=True, skip_group_check=True))
                                for vc in range(2)], reads=[sk, ("QH", h)], writes=[("PS", bo)])
                P.op("act", lambda e: e.activation(out=O[:, 2 * h:2 * h + 2, tsl], in_=po[:, 0:2, :], func=AF.Copy), reads=[("PS", bo)], writes=[("O", h)])

            pending = None
            for tt in range(8):
                tsl = slice(tt * 128, (tt + 1) * 128)
                ba = P.bank()
                pa = self.psb(ba)[:, 0:128]
                P.mm_group([lambda e, pa=pa, tsl=tsl: e.matmul(pa, lhsT=KT[:, tsl], rhs=QH[:, h, tsl], start=True, stop=True)],
                           reads=["KT", ("QH", h)], writes=[("PS", ba)])
                am = AM[tt % 2]
                P.op("dve", lambda e, pa=pa, am=am: e.tensor_tensor(out=am, in0=pa, in1=MASK, op=ALU.mult), reads=[("PS", ba), "CF"], writes=[("AM", tt % 2)])
                bo = P.bank()
                po = self.psb(bo).rearrange("p (a b) -> p a b", a=4)
                P.mm_group([(lambda e, vc=vc, po=po, tt=tt, am=am: e.matmul(po[:, vc, :], lhsT=VH[:, tt, vc * 128:(vc + 1) * 128], rhs=am, start=(vc == 0), stop=False, skip_group_check=True))
                            for vc in range(2)], reads=["VH", ("AM", tt % 2)], writes=[("PS", bo)])
                for hh in range(2):
                    j = 2 * tt + hh
                    bk = P.bank()
                    pkv = self.psb(bk)[:, 0:256]
                    P.mm_group([lambda e, pkv=pkv, hh=hh, tt=tt: e.matmul(pkv, lhsT=KTM[hh * 64:(hh + 1) * 64, tt, :], rhs=VH[hh * 64:(hh + 1) * 64, tt, :], start=True, stop=True)],
                               reads=["KTM", "VH"], writes=[("PS", bk)])
                    ttb = TTR[j % 2]
                    tk = ("TT", j % 2)
                    P.op("dve", lambda e, pkv=pkv, j=j, ttb=ttb: e.tensor_scalar(out=ttb, in0=pkv, scalar1=EBL[:, j:j + 1], scalar2=None, op0=ALU.mult),
                         reads=[("PS", bk), "EBL"], writes=[tk])
                    P.op("dve", lambda e, j=j, ttb=ttb: e.scalar_tensor_tensor(out=S32, in0=S32, scalar=EBL[:, j:j + 1], in1=ttb, op0=ALU.mult, op1=ALU.add),
                         reads=["S32", tk, "EBL"], writes=["S32"])
                    if j < 15:
                        sb = SBFR[j % 8]
                        P.op("act", lambda e, sb=sb: e.activation(out=sb, in_=S32, func=AF.Copy), reads=["S32"], writes=[("SBF", j % 8)])
                if pending is not None:
                    finish_tile(*pending)
                pending = (tt, po, bo)
            finish_tile(*pending)
            if DBG == 4:
                return
            P.op("act", lambda e: e.activation(out=SE[:, h, :], in_=S32, func=AF.Copy), reads=["S32"], writes=["SE"])
            P.op("dve", lambda e: e.memset(EC[:, 0:1], 1.0), writes=["EC"])
            for j in range(1, 16):
                P.op("dve", lambda e, j=j: e.tensor_tensor(out=EC[:, j:j + 1], in0=EC[:, j - 1:j], in1=EBL[:, j - 1:j], op=ALU.mult), reads=["EC", "EBL"], writes=["EC"])
            for j in range(1, 16):
                cs = slice(j * 64, (j + 1) * 64)
                P.op("dve", lambda e, j=j, cs=cs: e.tensor_scalar(out=QH[:, h, cs], in0=QH[:, h, cs], scalar1=EC[:, j:j + 1], scalar2=None, op0=ALU.mult),
                     reads=["EC", ("QH", h)], writes=[("QH", h)])

        for h in range(4):
            w, kw = self.ring_get()
            gla_head(h, w, kw)
            if DBG in (1, 2, 3, 4):
                return
        if DBG == 5:
            return
        ex_s = self.xt("ex1_s", [128, 1024], F32, True)
        P.dma("sp", out=ex_s, in_=SE.rearrange("p a b -> p (a b)"), reads=["SE"], writes=["d_ex1_s"], ds=self.ds("ex1s"))
        self._coll("ex1_s", [128, 1024], F32, ["d_ex1_s"], "d_ex1_sg")
        P.barrier()
        self.dbg_dump("LA", LA.rearrange("p a b -> p (a b)"), [128, 1024], F32)
        self.dbg_dump("ENB", ENB.rearrange("p a b -> p (a b)"), [128, 1024], F32)
        self.dbg_dump("EBT", EBT, [128, 1024], F32)
        self.dbg_dump("KT", KT, [128, 1024], BF16)
        self.dbg_dump("KTM", KTM.rearrange("p a b -> p (a b)"), [128, 1024], BF16)
        self.dbg_dump("VH", VH.rearrange("p a b -> p (a b)"), [128, 2048], BF16)
        self.dbg_dump("O", O.rearrange("p a b -> p (a b)"), [128, 8192], BF16)
        self.dbg_dump("QH", QH.rearrange("p a b -> p (a b)"), [128, 4096], BF16)
        self.dbg_dump("SE", SE.rearrange("p a b -> p (a b)"), [128, 1024], F32)
        self.dbg_dump("GZ", GZ, [32, 1024], BF16)
        self.dbg_dump("GW", GW, [32, 512], BF16)
        self.dbg_dump("EBL", EBL, [128, 16], F32)
        P.barrier()
        U, SR = L["U"], L["SR"]
        TMP = [self.av(L["X0"] + i * 2048, [128, 512], F32) for i in range(2)]
        ti = 0
        for t2 in range(2):
            w, kw = self.ring_get()
            for c in range(4):
                cs = slice(c * 128, (c + 1) * 128)
                for th in range(2):
                    ts = slice(th * 512, (th + 1) * 512)
                    b = P.bank()
                    ps = self.psb(b)
                    P.mm_group([(lambda e, kc=kc, ps=ps, cs=cs, ts=ts, w=w: e.matmul(ps, lhsT=w[:, kc, cs], rhs=HT[:, kc, ts], start=(kc == 0), stop=(kc == KC - 1)))
                                for kc in range(KC)], reads=[kw, ("HT", th)], writes=[("PS", b)])
                    P.op("act", lambda e, ps=ps, t2=t2, c=c, ts=ts: e.activation(out=SR[:, t2 * 4 + c, ts], in_=ps, func=AF.Silu), reads=[("PS", b)], writes=[("SR", t2 * 4 + c)])
            self.ring_done(1)
        for cb in range(2):
            wa, ka = self.ring_get()
            wb, kb = self.ring_get()
            for c in range(4):
                cs = slice(c * 128, (c + 1) * 128)
                for th in range(2):
                    ts = slice(th * 512, (th + 1) * 512)
                    b1 = P.bank()
                    b2 = P.bank()
                    p1, p2 = self.psb(b1), self.psb(b2)
                    P.mm_group([(lambda e, kc=kc, p1=p1, cs=cs, ts=ts, wa=wa: e.matmul(p1, lhsT=wa[:, kc, cs], rhs=HT[:, kc, ts], start=(kc == 0), stop=(kc == KC - 1)))
                                for kc in range(KC)], reads=[ka, ("HT", th)], writes=[("PS", b1)])
                    P.mm_group([(lambda e, kc=kc, p2=p2, cs=cs, ts=ts, wb=wb: e.matmul(p2, lhsT=wb[:, kc, cs], rhs=HT[:, kc, ts], start=(kc == 0), stop=(kc == KC - 1)))
                                for kc in range(KC)], reads=[kb, ("HT", th)], writes=[("PS", b2)])
                    tmp = TMP[ti % 2]
                    tk = ("TMP", ti % 2)
                    ti += 1
                    P.op("act", lambda e, tmp=tmp, p2=p2: e.activation(out=tmp, in_=p2, func=AF.Sigmoid), reads=[("PS", b2)], writes=[tk])
                    P.op("dve", lambda e, tmp=tmp, p1=p1, cb=cb, c=c, th=th: e.tensor_tensor(out=U[:, cb * 4 + c, 30 + th * 512:30 + (th + 1) * 512], in0=tmp, in1=p1, op=ALU.mult),
                         reads=[tk, ("PS", b1)], writes=[("U", cb * 4 + c)])
            self.ring_done(2)
        ex_u = self.xt("ex1_u", [128, 240], BF16, True)
        P.dma("sp", out=ex_u.rearrange("p (c w) -> p c w", c=8), in_=U[:, :, 1024:1054], reads=[("U", c) for c in range(8)], writes=["d_ex1_u"], ds=self.ds("ex1u"))

    def _outproj_prefetch(self, wname):
        Wo = self.din(wname, [D, D]).rearrange("(kc p) n -> p kc n", p=128)
        self.ring_begin(2, [[(Wo[:, :, t * 512:(t + 1) * 512], 16, 512)] for t in range(4)])

    def _outproj(self, wname, prefetched=False):
        P = self.P
        XT, HT = self.XT, self.HT
        if not prefetched:
            self._outproj_prefetch(wname)
        for t in range(4):
            w, kw = self.ring_get()
            for cc in range(4):
                dmc = 4 * t + cc
                cs = slice(cc * 128, (cc + 1) * 128)
                for th in range(2):
                    ts = slice(th * 512, (th + 1) * 512)
                    b = P.bank()
                    ps = self.psb(b)
                    P.mm_group([(lambda e, kc=kc, ps=ps, cs=cs, ts=ts, w=w: e.matmul(ps, lhsT=w[:, kc, cs], rhs=HT[:, kc, ts], start=(kc == 0), stop=(kc == KC - 1)))
                                for kc in range(KC)], reads=[kw], writes=[("PS", b)])
                    P.op("dve", lambda e, ps=ps, dmc=dmc, ts=ts: e.tensor_tensor(out=XT[:, dmc, ts], in0=XT[:, dmc, ts], in1=ps, op=ALU.add),
                         reads=[("PS", b), ("XT", dmc, th)], writes=[("XT", dmc, th)])
            self.ring_done(1)

    def ph_mixer0_b(self):
        P = self.P
        if not P.active:
            return
        P.barrier()
        HT, VEC, FL = self.HT, self.VEC, self.FL
        L = self._m0_layout()
        O, QH, SE, U, SR = L["O"], L["QH"], L["SE"], L["U"], L["SR"]
        X0 = L["X0"]
        SIN = SE.rearrange("p a b -> p (a b)")
        SINB = self.av(X0, [128, 4, 256], BF16)
        UT = self.av(X0 + 2048, [128, 8, 30], BF16)
        OT = self.av(0, [128, 2, 512], F32)
        TM = [self.av(4096 + i * 2048, [128, 512], F32) for i in range(2)]
        ACC = [self.av(8192 + i * 4096, [128, TOK], F32) for i in range(2)]
        QB = 2 * 16384 + 16384
        MEAN = self.av(QB, [128, 512], F32)
        M2 = self.av(QB + 2048, [128, 512], F32)
        RS2 = self.av(QB + 4096, [128, 512], F32)
        T1 = [self.av(QB + 6144 + i * 2048, [128, 512], F32) for i in range(2)]
        SQL = [self.av(QB + 10240 + i * 1024, [128, 512], BF16) for i in range(2)]
        xs = self.halo_src("ex1_s", [128, 1024], F32)
        xu = self.halo_src("ex1_u", [128, 240], BF16)
        P.dma("sp", out=SIN, in_=xs, reads=["d_ex1_sg"], writes=["SIN"], ds=self.ds("sin"))
        P.dma("sp", out=UT, in_=xu.rearrange("p (c w) -> p c w", c=8), reads=["d_ex1_ug"], writes=["UT"], ds=self.ds("ut"))
        P.op("dve", lambda e: e.tensor_scalar(out=SINB.rearrange("p a b -> p (a b)"), in0=SIN, scalar1=FL[:, 0:1], scalar2=None, op0=ALU.mult), reads=["SIN", "FL"], writes=["SINB"])
        P.op("dve", lambda e: e.tensor_scalar(out=U[:, :, 0:30], in0=UT, scalar1=FL[:, 0:1], scalar2=None, op0=ALU.mult), reads=["UT", "FL"], writes=["UH"])
        ti = 0
        for h in range(4):
            for th in range(2):
                ts = slice(th * 512, (th + 1) * 512)
                for vc in range(2):
                    b = P.bank()
                    ps = self.psb(b)
                    P.mm_group([lambda e, ps=ps, vc=vc, ts=ts, h=h: e.matmul(ps, lhsT=SINB[:, h, vc * 128:(vc + 1) * 128], rhs=QH[:, h, ts], start=True, stop=True)],
                               reads=["SINB"], writes=[("PS", b)])
                    P.op("dve", lambda e, ps=ps, vc=vc, ts=ts, h=h: e.tensor_tensor(out=OT[:, vc, :], in0=ps, in1=O[:, 2 * h + vc, ts], op=ALU.add),
                         reads=[("PS", b)], writes=[("OT", vc)])
                RSTD = self._stats(th, lambda c0, n, th_: OT[:, c0:c0 + n, :], 2, 1.0 / 256, [("OT", 0), ("OT", 1)])
                for vc in range(2):
                    tm = TM[ti % 2]
                    tk = ("TM", ti % 2)
                    ti += 1
                    P.op("dve", lambda e, tm=tm, vc=vc: e.scalar_tensor_tensor(out=tm, in0=OT[:, vc, :], scalar=VEC[:, R_GLN + vc:R_GLN + vc + 1], in1=RSTD, op0=ALU.mult, op1=ALU.mult),
                         reads=[("OT", vc), "RSTD"], writes=[tk])
                    P.op("dve", lambda e, tm=tm, vc=vc, ts=ts, h=h: e.tensor_tensor(out=HT[:, 2 * h + vc, ts], in0=tm, in1=SR[:, 2 * h + vc, ts], op=ALU.mult),
                         reads=[tk], writes=[("CC", 2 * h + vc, th)])
        P.barrier()
        YB = self.av(2 * 16384, [128, 8, TOK], BF16)
        DG = [self.av(L["T0"] + 17664 + i * 8192, [128, 31, 128], BF16) for i in range(2)]
        self._outproj_prefetch("ab_w_out")

        def conv_chunk(c):
            dg = DG[c % 2]
            kd, ka = ("DGd", c % 2), ("DGa", c % 2)
            for wi in range(31):
                col = VEC[:, R_DW + wi * 8 + c:R_DW + wi * 8 + c + 1]
                if wi % 2 == 0:
                    P.op("dve", lambda e, wi=wi, col=col: e.tensor_scalar(out=dg[:, wi, :], in0=self.IDENTB, scalar1=col, scalar2=None, op0=ALU.mult),
                         reads=["IDENTB"], writes=[kd])
                else:
                    P.op("act", lambda e, wi=wi, col=col: e.activation(out=dg[:, wi, :], in_=self.IDENTB, func=AF.Identity, scale=col),
                         reads=["IDENTB"], writes=[ka])
            for th in range(2):
                b = P.bank()
                ps = self.psb(b)
                P.mm_group([(lambda e, wi=wi, th=th, ps=ps: e.matmul(ps, lhsT=dg[:, wi, :], rhs=U[:, c, wi + th * 512:wi + th * 512 + 512], start=(wi == 0), stop=(wi == 30)))
                            for wi in range(31)], reads=[kd, ka, "UH"], writes=[("PS", b)])
                P.op("act", lambda e, th=th, ps=ps: e.activation(out=YB[:, c, th * 512:(th + 1) * 512], in_=ps, func=AF.Identity, bias=VEC[:, R_CB + c:R_CB + c + 1]),
                     reads=[("PS", b)], writes=[("YB", c, th)])

        for c in range(8):
            conv_chunk(c)
        ti = 0
        for th in range(2):
            ts = slice(th * 512, (th + 1) * 512)
            b1 = P.bank()
            b2 = P.bank()
            p1, p2 = self.psb(b1), self.psb(b2)
            for c in range(8):
                sq = SQL[c % 2]
                P.op("act", lambda e, sq=sq, c=c, ts=ts: e.activation(out=sq, in_=YB[:, c, ts], func=AF.Square), reads=[("YB", c, th)], writes=[("SQL", c % 2)])
                P.mm_group([lambda e, c=c, ts=ts, p1=p1: e.matmul(p1, lhsT=self.ONESB, rhs=YB[:, c, ts], start=(c == 0), stop=(c == 7), skip_group_check=True)],
                           reads=[("YB", c, th)], writes=[("PS", b1)])
                P.mm_group([lambda e, c=c, sq=sq, p2=p2: e.matmul(p2, lhsT=self.ONESB, rhs=sq, start=(c == 0), stop=(c == 7), skip_group_check=True)],
                           reads=[("SQL", c % 2)], writes=[("PS", b2)])
            P.op("act", lambda e, p1=p1: e.activation(out=MEAN, in_=p1, func=AF.Identity, scale=1.0 / 1024), reads=[("PS", b1)], writes=["MEAN"])
            P.op("dve", lambda e: e.tensor_tensor(out=M2, in0=MEAN, in1=MEAN, op=ALU.mult), reads=["MEAN"], writes=["M2"])
            P.op("dve", lambda e, p2=p2: e.scalar_tensor_tensor(out=M2, in0=p2, scalar=1.0 / 1024, in1=M2, op0=ALU.mult, op1=ALU.subtract), reads=[("PS", b2), "M2"], writes=["M2"])
            P.op("act", lambda e: e.activation(out=RS2, in_=M2, func=AF.Sqrt, bias=EPS, scale=1.0), reads=["M2"], writes=["RS2"])
            P.op("dve", lambda e: e.reciprocal(out=RS2, in_=RS2), reads=["RS2"], writes=["RS2"])
            for c in range(8):
                t1 = T1[ti % 2]
                tk = ("T1", ti % 2)
                ti += 1
                P.op("dve", lambda e, t1=t1, c=c, ts=ts: e.tensor_tensor(out=t1, in0=YB[:, c, ts], in1=MEAN, op=ALU.subtract), reads=[("YB", c, th), "MEAN"], writes=[tk])
                P.op("dve", lambda e, t1=t1: e.tensor_tensor(out=t1, in0=t1, in1=RS2, op=ALU.mult), reads=[tk, "RS2"], writes=[tk])
                P.op("act", lambda e, t1=t1, c=c, ts=ts: e.activation(out=HT[:, 8 + c, ts], in_=t1, func=AF.Silu, bias=VEC[:, R_LNB + c:R_LNB + c + 1], scale=VEC[:, R_LNG + c:R_LNG + c + 1]),
                     reads=[tk], writes=[("CC", 8 + c, th)])
        P.barrier()
        z = os.environ.get("K_ZERO", "")
        if z == "A":
            P.op("dve", lambda e: e.memset(HT[:, 0:8, :], 0.0), writes=["zz"])
        if z == "B":
            P.op("dve", lambda e: e.memset(HT[:, 8:16, :], 0.0), writes=["zz"])
        P.barrier()
        self._outproj("ab_w_out", prefetched=True)

    def ph_attn_a(self, norm_row):
        P = self.P
        if not P.active:
            return
        P.barrier()
        HT = self.HT
        Wqkv = self.din("att_w_qkv", [D, 3 * D]).rearrange("(kc p) n -> p kc n", p=128)
        self.ring_begin(2, [[(Wqkv[:, :, t * 512:(t + 1) * 512], 16, 512)] for t in (4, 5, 6, 7, 8, 9, 10, 11, 0, 1, 2, 3)])
        self._norm(norm_row)
        R0 = 2 * 16384
        QT = self.av(R0, [128, 16, TOK], BF16)
        KS = [self.av(R0 + 32768 + i * 2048, [128, TOK], BF16) for i in range(2)]
        VS = [self.av(R0 + 36864 + i * 1024, [128, 512], BF16) for i in range(4)]
        kT_own = self.xt("kT_own", [D, TOK], BF16, True)
        v_own = self.xt("v_own", [TOK, D], BF16, True)
        kT_tail = self.xt("kT_tail", [D, 512], BF16, True)
        v_tail = self.xt("v_tail", [512, D], BF16, True)

        def evac(eng, dst, ps, b, wkeys):
            if eng == "dve":
                P.op("dve", lambda e: e.tensor_copy(out=dst, in_=ps), reads=[("PS", b)], writes=wkeys)
            else:
                P.op("act", lambda e: e.activation(out=dst, in_=ps, func=AF.Copy), reads=[("PS", b)], writes=wkeys)

        def proj_fm(w, kw, c, th, dst, eng, wkeys):
            ts = slice(th * 512, (th + 1) * 512)
            cs = slice(c * 128, (c + 1) * 128)
            b = P.bank()
            ps = self.psb(b)
            P.mm_group([(lambda e, kc=kc: e.matmul(ps, lhsT=w[:, kc, cs], rhs=HT[:, kc, ts], start=(kc == 0), stop=(kc == KC - 1))) for kc in range(KC)],
                       reads=[kw, ("HT", th)], writes=[("PS", b)])
            evac(eng, dst, ps, b, wkeys)

        for t in range(4):
            w, kw = self.ring_get()
            for c in range(4):
                h = 4 * t + c
                ks = KS[h % 2]
                for th in range(2):
                    proj_fm(w, kw, c, th, ks[:, th * 512:(th + 1) * 512], "dve" if h % 2 == 0 else "act", [("KS", h % 2)])
                P.dma("sp", out=kT_own[h * 128:(h + 1) * 128, :], in_=ks, reads=[("KS", h % 2)], writes=[("d_kTo", h)], ds=self.ds("ks%d" % (h % 2)))
                P.dma("sp", out=kT_tail[h * 128:(h + 1) * 128, :], in_=ks[:, 512:1024], reads=[("KS", h % 2)], writes=[("d_kTt", h)], ds=self.ds("ks%d" % (h % 2)))
            self.ring_done(1)
        self._coll("kT_tail", [D, 512], BF16, [("d_kTt", h) for h in range(16)], "d_halo_k")
        vi = 0
        for t in range(4):
            w, kw = self.ring_get()
            for tt in range(8):
                tsl = slice(tt * 128, (tt + 1) * 128)
                b = P.bank()
                ps = self.psb(b)

                def vgroup(w=w, tsl=tsl, ps=ps):
                    return [(lambda e, kc=kc: e.matmul(ps, lhsT=HT[:, kc, tsl], rhs=w[:, kc, :], start=(kc == 0), stop=(kc == KC - 1))) for kc in range(KC)]
                P.mm_group(vgroup(), reads=[kw, ("HT", tt // 4)], writes=[("PS", b)])
                vs = VS[vi % 4]
                vk = ("VS", vi % 4)
                evac("dve" if vi % 2 == 0 else "act", vs, ps, b, [vk])
                P.dma("sp", out=v_own[tsl, t * 512:(t + 1) * 512], in_=vs, reads=[vk], writes=[("d_vo", t, tt)], ds=self.ds("vs%d" % (vi % 4)))
                if tt >= 4:
                    P.dma("sp", out=v_tail[(tt - 4) * 128:(tt - 3) * 128, t * 512:(t + 1) * 512], in_=vs, reads=[vk], writes=[("d_vt", t, tt)], ds=self.ds("vs%d" % (vi % 4)))
                vi += 1
            self.ring_done(1)
        self._coll("v_tail", [512, D], BF16, [("d_vt", t, tt) for t in range(4) for tt in range(4, 8)], "d_halo_v")
        for t in range(4):
            w, kw = self.ring_get()
            for c in range(4):
                h = 4 * t + c
                for th in range(2):
                    proj_fm(w, kw, c, th, QT[:, h, th * 512:(th + 1) * 512], "dve" if h % 2 == 0 else "act", [("QT", h)])
            self.ring_done(1)

    def ph_attn_b(self):
        P = self.P
        if not P.active:
            return
        P.barrier()
        HT, FL = self.HT, self.FL
        R0 = 2 * 16384
        QT = self.av(R0, [128, 16, TOK], BF16)
        o = R0 + 32768
        KH = [self.av(o + i * 3072, [128, 1536], BF16) for i in range(2)]
        VH = [self.av(o + 6144 + i * 3072, [128, 12, 128], BF16) for i in range(2)]
        BH = [self.av(o + 12288 + i * 2560, [128, 640], F32) for i in range(2)]
        kT_own = self.xt("kT_own", [D, TOK], BF16, False)
        v_own = self.xt("v_own", [TOK, D], BF16, False)
        kT_halo = self.halo_src("kT_tail", [D, 512], BF16)
        v_halo = self.halo_src("v_tail", [512, D], BF16)
        biasT = self.din("biasT", [16 * 128, 640])
        scale = 128.0 ** -0.5
        self._outproj_prefetch("att_w_o")
        TMPS = [self.av(o + 17408 + i * 2560, [128, 640], F32) for i in range(3)]
        PTS = [self.av(o + 25088 + i * 1280, [128, 640], BF16) for i in range(3)]
        RL = [self.av(o + 28928 + i * 512, [128, 128], F32) for i in range(2)]

        def load(h):
            s = h % 2
            hd = ("HD", s)
            dsx = self.ds("hd%d" % s)
            hs = slice(h * 128, (h + 1) * 128)
            rd = [("d_kTo", h), "d_halo_k", "d_halo_v"] + [("d_vo", h // 4, tt) for tt in range(8)]
            P.dma("sp", out=KH[s][:, 0:512], in_=kT_halo[hs, :], reads=rd, writes=[hd], ds=dsx)
            P.dma("sp", out=KH[s][:, 512:1536], in_=kT_own[hs, :], reads=rd, writes=[hd], ds=dsx)
            P.dma("sp", out=VH[s][:, 0:4, :], in_=v_halo[:, hs].rearrange("(j p) c -> p j c", p=128), reads=rd, writes=[hd], ds=dsx)
            P.dma("sp", out=VH[s][:, 4:12, :], in_=v_own[:, hs].rearrange("(j p) c -> p j c", p=128), reads=rd, writes=[hd], ds=dsx)
            P.dma("sp", out=BH[s], in_=biasT[hs, :], reads=rd, writes=[hd], ds=dsx)

        def stage1(idx, h, n):
            s = h % 2
            hd = ("HD", s)
            qsl = slice(n * 128, (n + 1) * 128)
            k = idx % 3
            bA = P.bank()
            bB = P.bank()
            pA, pB = self.psb(bA), self.psb(bB)
            mms = [(lambda e, i=i: e.matmul(pA[:, i * 128:(i + 1) * 128], lhsT=KH[s][:, (n + i) * 128:(n + i + 1) * 128], rhs=QT[:, h, qsl], start=True, stop=True))
                   for i in range(4)]
            mms.append(lambda e: e.matmul(pB[:, 0:128], lhsT=KH[s][:, (n + 4) * 128:(n + 5) * 128], rhs=QT[:, h, qsl], start=True, stop=True))
            P.mm_group(mms, reads=[hd, ("QT", h)], writes=[("PS", bA), ("PS", bB)])
            tmp = TMPS[k]
            tk = ("TMPS", k)
            P.op("dve", lambda e: e.scalar_tensor_tensor(out=tmp[:, 0:512], in0=pA, scalar=scale, in1=BH[s][:, 0:512], op0=ALU.mult, op1=ALU.add),
                 reads=[("PS", bA), hd], writes=[tk])
            P.op("dve", lambda e: e.scalar_tensor_tensor(out=tmp[:, 512:640], in0=pB[:, 0:128], scalar=scale, in1=BH[s][:, 512:640], op0=ALU.mult, op1=ALU.add),
                 reads=[("PS", bB), hd], writes=[tk])
            pt = PTS[k]
            pk = ("PTS", k)
            if n < 4:
                nh = (4 - n) * 128
                P.op("act", lambda e: e.activation(out=pt[:, 0:nh], in_=tmp[:, 0:nh], func=AF.Exp, bias=FL[:, 1:2]), reads=[tk, "FL"], writes=[pk])
                P.op("act", lambda e: e.activation(out=pt[:, nh:640], in_=tmp[:, nh:640], func=AF.Exp), reads=[tk], writes=[pk])
            else:
                P.op("act", lambda e: e.activation(out=pt, in_=tmp, func=AF.Exp), reads=[tk], writes=[pk])

        def stage2(idx, h, n):
            s = h % 2
            hd = ("HD", s)
            qsl = slice(n * 128, (n + 1) * 128)
            k = idx % 3
            pt = PTS[k]
            pk = ("PTS", k)
            bO = P.bank()
            pO = self.psb(bO)
            mm2 = [(lambda e, i=i: e.matmul(pO[:, 0:128], lhsT=VH[s][:, n + i, :], rhs=pt[:, i * 128:(i + 1) * 128], start=(i == 0), stop=(i == 4), skip_group_check=True))
                   for i in range(5)]
            mm2 += [(lambda e, i=i: e.matmul(pO[:, 128:256], lhsT=self.ONESB, rhs=pt[:, i * 128:(i + 1) * 128], start=False, stop=(i == 4), skip_group_check=True))
                    for i in range(5)]
            P.mm_group(mm2, reads=[pk, hd], writes=[("PS", bO)])
            rl = RL[idx % 2]
            rk = ("RL", idx % 2)
            P.op("dve", lambda e: e.reciprocal(out=rl, in_=pO[:, 128:256]), reads=[("PS", bO)], writes=[rk])
            P.op("dve", lambda e: e.tensor_tensor(out=HT[:, h, qsl], in0=pO[:, 0:128], in1=rl, op=ALU.mult), reads=[("PS", bO), rk], writes=[("CC", h, n)])

        items = [(h, n) for h in range(16) for n in range(8)]
        load(0)
        load(1)
        for idx, (h, n) in enumerate(items):
            stage1(idx, h, n)
            if idx >= 1:
                hp, np_ = items[idx - 1]
                stage2(idx - 1, hp, np_)
                if np_ == 7 and hp + 2 < 16:
                    load(hp + 2)
        stage2(len(items) - 1, *items[-1])
        P.barrier()
        self._outproj("att_w_o", prefetched=True)

    def _state_list(self):
        return [("st_xt", self.XT.rearrange("p a b -> p (a b)"), [128, KC * TOK], F32),
                ("st_ht", self.HT.rearrange("p a b -> p (a b)"), [128, KC * TOK], BF16),
                ("st_ar", self.AR, [128, self.ARB // 2], BF16),
                ("st_cf", self.CF, [128, 512], F32),
                ("st_vec", self.VEC, [128, NVROW], F32),
                ("st_ones", self.ONESB, [128, 128], BF16),
                ("st_identb", self.IDENTB, [128, 128], BF16),
                ("st_fl", self.FL, [128, 4], F32)]

    def dump_state(self):
        P = self.P
        P.wait_all("sp")
        for name, ap, shape, dt in self._state_list():
            t = self.dout(name + "_o", shape, dt)
            P.dma("sp", out=t, in_=ap, writes=[("dump", name)], ds=self.ds("dump_" + name))

    def restore_state(self):
        P = self.P
        for name, ap, shape, dt in self._state_list():
            t = self.din(name + "_i", shape, dt)
            P.dma("sp", out=ap, in_=t, writes=[("rest", name)], ds=self.ds("rest_" + name))
        P.barrier()

    def finish(self):
        P = self.P
        P.active = True
        P.wait_all("sp")
        P.emit()


def build_program(seg=None, upto=None, dbg_raw=False):
    B = Builder(seg)
    plan = [
        (0, lambda: B.ph_consts()),
        (0, lambda: B.ph_load_x()),
        (0, lambda: B.ph_ffn(0, R_FFN + 0)),
        (0, lambda: B.ph_mixer0_a(R_MIX + 0)),
        (1, lambda: B.ph_mixer0_b()),
        (1, lambda: B.ph_ffn(1, R_FFN + 16)),
        (1, lambda: B.ph_pl(0, R_PL + 0)),
        (1, lambda: B.ph_ffn(2, R_FFN + 32)),
        (1, lambda: B.ph_attn_a(R_MIX + 16)),
        (2, lambda: B.ph_attn_b()),
        (2, lambda: B.ph_ffn(3, R_FFN + 48)),
        (2, lambda: B.ph_pl(1, R_PL + 16)),
    ]
    if upto is not None:
        plan = plan[:upto]
    B.n_seg = plan[-1][0] + 1
    last_seg = 0
    for sg, fn in plan:
        if sg != last_seg:
            B.exchange(last_seg)
            last_seg = sg
        fn()
    if upto is not None:
        if dbg_raw:
            B.ph_dbg_ht()
        B.ph_final(normalize=False)
    else:
        B.ph_final(normalize=True)
    B.finish()
    return B


def _consts():
    c = np.zeros((128, 512), np.float32)
    c[:, 0:128] = np.eye(128, dtype=np.float32)
    s = np.arange(128)[:, None]
    t = np.arange(128)[None, :]
    m = ((s <= t) & ((s // 64) == (t // 64))).astype(np.float32)
    c[:, 128:256] = m
    c[:, 256:384] = m * np.float32(-1.0 / 16.0)
    c[:, 384:512] = 1.0
    return c


def _bias_index():
    sl = np.arange(128)[:, None, None]
    i = np.arange(5)[None, :, None]
    tl = np.arange(128)[None, None, :]
    delta = 4 - i
    rel = delta * 128 + tl - sl
    idx = np.clip(rel, -128, 128) + 128
    dc = 2 * delta + (tl >= 64).astype(np.int64) - (sl >= 64).astype(np.int64)
    valid = (dc >= 0) & (dc <= 8)
    return idx, valid


def host_inputs(inp):
    f = np.float32
    g = lambda k: np.asarray(inp[k], dtype=f)
    vec = np.zeros((NVROW, 128), f)
    vec[R_FFN:R_FFN + 64] = g("ffn_norm").reshape(64, 128)
    vec[R_MIX:R_MIX + 32] = g("mix_norm").reshape(32, 128)
    vec[R_PL:R_PL + 32] = g("pl_norm").reshape(32, 128)
    vec[R_FIN:R_FIN + 16] = g("final_norm").reshape(16, 128)
    vec[R_GLN:R_GLN + 2] = g("gla_norm_g").reshape(2, 128)
    vec[R_CB:R_CB + 8] = g("conv_dw_b").reshape(8, 128)
    vec[R_LNG:R_LNG + 8] = g("conv_ln_g").reshape(8, 128)
    vec[R_LNB:R_LNB + 8] = g("conv_ln_b").reshape(8, 128)
    vec[R_DW:R_DW + 248] = g("conv_dw").reshape(248, 128)
    gw = np.concatenate([g("gla_gate_w").reshape(16, 512), g("gla_gate_b").reshape(1, 512)], axis=0)
    idx, valid = _bias_index()
    rb = g("att_rel_bias").reshape(16, 257)
    biasT = np.where(valid[None], rb[:, idx], f(NEG)).astype(f).reshape(16 * 128, 640)
    shared = {
        "consts": _consts(), "vecs": vec, "gwaug": np.ascontiguousarray(gw), "biasT": np.ascontiguousarray(biasT),
        "ffn_w_gate": g("ffn_w_gate").reshape(4 * D, FF), "ffn_w_up": g("ffn_w_up").reshape(4 * D, FF),
        "ffn_w_down": g("ffn_w_down").reshape(4 * FF, D),
        "ab_w_in": g("ab_w_in").reshape(D, ABIN), "ab_w_out": g("ab_w_out").reshape(D, D),
        "att_w_qkv": g("att_w_qkv").reshape(D, 3 * D), "att_w_o": g("att_w_o").reshape(D, D),
        "pl_w_gate": g("pl_w_gate").reshape(2 * D, D), "pl_w_proj": g("pl_w_proj").reshape(2 * 256, D),
    }
    x = g("x")
    p = g("p")
    per = []
    for c in range(NCORES):
        b, h = c // 2, c % 2
        fl = np.zeros((128, 4), f)
        fl[:, 0] = float(h)
        fl[:, 1] = 0.0 if h == 1 else NEG
        dct = dict(shared)
        dct["x"] = np.ascontiguousarray(x[b, h * TOK:(h + 1) * TOK, :])
        dct["p"] = np.ascontiguousarray(p[:, b, h * TOK:(h + 1) * TOK, :]).reshape(2 * TOK, 256)
        dct["flags"] = fl
        per.append(dct)
    return per


def run_chain(inp, upto=None, ncore=NCORES, trace=False):
    host = host_inputs(inp)
    carry = [dict() for _ in range(ncore)]
    sg = 0
    times = []
    while True:
        B = build_program(seg=sg, upto=upto)
        in_maps = []
        for c in range(ncore):
            m = {}
            for n in B.in_names:
                m[n] = host[c][n] if n in host[c] else carry[c][n]
            in_maps.append(m)
        res = run_bass_kernel_spmd(B.nc, in_maps, core_ids=list(range(ncore)), trace=trace)
        times.append(res.exec_time_ns)
        if sg == B.n_seg - 1:
            return [res.results[c]["out"] for c in range(ncore)], times
        for c in range(ncore):
            carry[c] = {}
            for name, val in res.results[c].items():
                if name.endswith("_o"):
                    carry[c][name[:-2] + "_i"] = val
            first = res.results[2 * (c // 2)]
            for name, val in first.items():
                if name.endswith("_o"):
                    carry[c][name[:-2] + "_h"] = val
        sg += 1


def run_fused(inp, ncore=NCORES, trace=False):
    host = host_inputs(inp)
    B = build_program(seg=None)
    in_maps = [{n: host[c][n] for n in B.in_names} for c in range(ncore)]
    res = run_bass_kernel_spmd(B.nc, in_maps, core_ids=list(range(ncore)), trace=trace)
    return [res.results[c]["out"] for c in range(ncore)], [res.exec_time_ns]


def kernel(**inputs):
    if MODE == "fused":
        outs, _ = run_fused(inputs)
    else:
        outs, _ = run_chain(inputs)
    full = np.zeros((4, 2 * TOK, D), np.float32)
    for c in range(NCORES):
        full[c // 2, (c % 2) * TOK:(c % 2 + 1) * TOK, :] = outs[c]
    return full
```

```python
import os
import numpy as np
import concourse.bass as bass
import concourse.mybir as mybir
from concourse.bass_utils import run_bass_kernel_spmd

F32 = mybir.dt.float32
BF16 = mybir.dt.bfloat16
AF = mybir.ActivationFunctionType
ALU = mybir.AluOpType

NCORES = 8
TOK = 1024
D = 2048
KC = 16
FF = 5632
NGRP = 11
EPS = 1e-6
ABIN = 5136
NEG = -30000.0

R_FFN = 0
R_MIX = 64
R_PL = 96
R_FIN = 128
R_GLN = 144
R_CB = 146
R_LNG = 154
R_LNB = 162
R_DW = 170
NVROW = 512

MODE = "fused"
USE_POOL = False


class _Src:
    def __init__(self, name, sem):
        self.name = name
        self.sem = sem
        self.count = 0


class Prog:
    ENG = ("pe", "act", "dve", "pool", "sp")

    def __init__(self, nc):
        self.nc = nc
        self.src = {n: _Src(n, nc.alloc_semaphore("e_" + n)) for n in self.ENG}
        self.ops = {n: [] for n in self.ENG}
        self.seen = {n: {} for n in self.ENG}
        self.bufs = {}
        self.dss = []
        self.active = True
        self.bank_i = 0
        self.bank_mod = 8
        self.nobar = set()
        self.keep_keys = set()

    def new_ds(self, name):
        s = _Src(name, self.nc.alloc_semaphore("d%d_%s" % (len(self.dss), name)))
        self.dss.append(s)
        return s

    def _wait(self, eng, s, v):
        seen = self.seen[eng]
        if seen.get(s, 0) >= v:
            return
        seen[s] = v
        self.ops[eng].append(lambda e, sem=s.sem, val=v: e.wait_ge(sem, val))

    def _wait_deps(self, eng, reads, writes):
        deps = {}
        me = self.src[eng]
        for k in reads:
            b = self.bufs.get(k)
            if b and b[0]:
                s, v = b[0]
                if v > deps.get(s, 0):
                    deps[s] = v
        for k in writes:
            b = self.bufs.get(k)
            if b:
                if b[0]:
                    s, v = b[0]
                    if v > deps.get(s, 0):
                        deps[s] = v
                for s, v in b[1].items():
                    if s is me:
                        continue
                    if v > deps.get(s, 0):
                        deps[s] = v
        for s, v in deps.items():
            if s is me and eng == "pe":
                continue
            self._wait(eng, s, v)

    def _record(self, src, val, reads, writes):
        for k in reads:
            b = self.bufs.setdefault(k, [None, {}])
            b[1][src] = val
        for k in writes:
            self.bufs[k] = [(src, val), {}]

    def op(self, eng, fn, reads=(), writes=()):
        if not self.active:
            return
        self._wait_deps(eng, reads, writes)
        s = self.src[eng]
        s.count += 1
        self.ops[eng].append(lambda e, f=fn, sem=s.sem: f(e).then_inc(sem, 1))
        self._record(s, s.count, reads, writes)

    def mm_group(self, mms, reads, writes):
        if not self.active:
            return
        self._wait_deps("pe", reads, writes)
        for f in mms[:-1]:
            self.ops["pe"].append(lambda e, f=f: f(e))
        s = self.src["pe"]
        s.count += 1
        self.ops["pe"].append(lambda e, f=mms[-1], sem=s.sem: f(e).then_inc(sem, 1))
        self._record(s, s.count, reads, writes)

    def dma(self, eng, out, in_, reads=(), writes=(), ds=None):
        if not self.active:
            return
        self._wait_deps(eng, reads, writes)
        ds.count += 16
        self.ops[eng].append(lambda e, o=out, i=in_, sem=ds.sem: e.dma_start(out=o, in_=i).then_inc(sem, 16))
        self._record(ds, ds.count, reads, writes)

    def barrier(self):
        if not self.active:
            return
        srcs = [x for x in list(self.src.values()) + self.dss if x not in self.nobar]
        for eng in self.ENG:
            for s in srcs:
                if s.count > 0 and not (s is self.src[eng]):
                    self._wait(eng, s, s.count)
                elif s.count > 0 and eng not in ("pe", "sp", "pool"):
                    self._wait(eng, s, s.count)
        self.bufs = {k: v for k, v in self.bufs.items() if k in self.keep_keys}

    def wait_all(self, eng):
        for s in list(self.src.values()) + self.dss:
            if s.count > 0 and s is not self.src[eng]:
                self._wait(eng, s, s.count)

    def bank(self):
        b = self.bank_i % self.bank_mod
        self.bank_i += 1
        return b

    def emit(self):
        nc = self.nc
        with nc.Block() as block:
            @block.tensor
            def _(e):
                for f in self.ops["pe"]:
                    f(e)

            @block.scalar
            def _(e):
                for f in self.ops["act"]:
                    f(e)

            @block.vector
            def _(e):
                for f in self.ops["dve"]:
                    f(e)

            @block.gpsimd
            def _(e):
                for f in self.ops["pool"]:
                    f(e)

            @block.sync
            def _(e):
                for f in self.ops["sp"]:
                    f(e)


class Builder:
    def __init__(self, seg, stop=None):
        self.seg = seg
        self.stop = stop
        nc = bass.Bass("TRN2", target_bir_lowering=False)
        self.nc = nc
        self.P = Prog(nc)
        self.dram = {}
        self.in_names = []
        self.out_names = []
        self.XT = nc.alloc_sbuf_tensor("XT", [128, KC, TOK], F32).ap()
        self.HT = nc.alloc_sbuf_tensor("HT", [128, KC, TOK], BF16).ap()
        self.CF = nc.alloc_sbuf_tensor("CF", [128, 512], F32).ap()
        self.VEC = nc.alloc_sbuf_tensor("VEC", [128, NVROW], F32).ap()
        self.ONESB = nc.alloc_sbuf_tensor("ONESB", [128, 128], BF16).ap()
        self.IDENTB = nc.alloc_sbuf_tensor("IDENTB", [128, 128], BF16).ap()
        self.FL = nc.alloc_sbuf_tensor("FL", [128, 4], F32).ap()
        rem = nc.sbuf_bytes_remaining
        self.ARB = (min(rem, 109 * 1024) // 1024) * 1024 - 1024
        self.AR = nc.alloc_sbuf_tensor("AR", [128, self.ARB // 2], BF16).ap()
        self.PS = nc.alloc_psum_tensor("PS", [128, 8, 512], F32).ap()
        self.o_sq = self.ARB - 10 * 1024
        self.o_rstd = self.ARB - 2 * 1024
        self.GEN_END = self.o_sq
        self.ring_ds = [self.P.new_ds("ring%d" % i) for i in range(5)]
        self.misc_ds = {}
        self.cur_seg = 0
        self._set_active()

    def _set_active(self):
        self.P.active = (self.seg is None or self.seg == self.cur_seg)

    def ds(self, name):
        if name not in self.misc_ds:
            self.misc_ds[name] = self.P.new_ds(name)
        return self.misc_ds[name]

    def din(self, name, shape, dtype=F32):
        if name not in self.dram:
            self.dram[name] = self.nc.dram_tensor(name, list(shape), dtype, kind="ExternalInput").ap()
            self.in_names.append(name)
        return self.dram[name]

    def dout(self, name, shape, dtype=F32):
        if name not in self.dram:
            self.dram[name] = self.nc.dram_tensor(name, list(shape), dtype, kind="ExternalOutput").ap()
            self.out_names.append(name)
        return self.dram[name]

    def dscratch(self, name, shape, dtype):
        if name not in self.dram:
            self.dram[name] = self.nc.dram_tensor(name, list(shape), dtype, kind="Internal").ap()
        return self.dram[name]

    def av(self, off, shape, dtype):
        esz = 4 if dtype == F32 else 2
        n = 1
        for s in shape[1:]:
            n *= s
        nb = n * esz
        assert off % 4 == 0 and off + nb <= self.ARB, (off, nb, self.ARB)
        v = self.AR[0:shape[0], off // 2:(off + nb) // 2]
        if dtype == F32:
            v = v.bitcast(F32)
        if len(shape) == 3:
            v = v.rearrange("p (a b) -> p a b", a=shape[1])
        return v

    def psb(self, b):
        return self.PS[:, b, :]

    def ring_begin(self, nslots, tiles):
        self.r_n = nslots
        self.r_tiles = tiles
        self.r_issued = 0
        self.r_next = 0
        self.r_done = 0
        self.r_views = {}
        self._ring_pump()

    def _ring_pump(self):
        lim = min(len(self.r_tiles), self.r_done + self.r_n)
        while self.r_issued < lim:
            self._ring_issue(self.r_issued)
            self.r_issued += 1

    def ring_done(self, k=1):
        self.r_done += k
        self._ring_pump()

    def _ring_issue(self, i):
        P = self.P
        s = i % self.r_n
        pieces = self.r_tiles[i]
        a = pieces[0][1]
        btot = sum(pc[2] for pc in pieces)
        assert a * btot * 2 <= 16384
        v = self.av(s * 16384, [128, a, btot], BF16)
        c0 = 0
        for (src, a_, b_) in pieces:
            P.dma("pool", out=v[:, :, c0:c0 + b_], in_=src, writes=[("ring", s)], ds=self.ring_ds[s])
            c0 += b_
        self.r_views[i] = (v, ("ring", s))

    def ring_get(self):
        i = self.r_next
        self.r_next += 1
        assert i < self.r_issued, "ring: tile not issued (missing ring_done?)"
        return self.r_views.pop(i)

    def ph_consts(self):
        P = self.P
        if not P.active:
            return
        consts = self.din("consts", [128, 512])
        vecs = self.din("vecs", [NVROW, 128])
        flags = self.din("flags", [128, 4])
        d = self.ds("c0")
        P.dma("sp", out=self.CF, in_=consts, writes=["CF"], ds=d)
        d2 = self.ds("c1")
        P.dma("sp", out=self.FL, in_=flags, writes=["FL"], ds=d2)
        vs = self.av(0, [128, 4, 128], F32)
        d3 = self.ds("c2")
        P.dma("sp", out=vs, in_=vecs.rearrange("(j p) c -> p j c", p=128), writes=["vs"], ds=d3)
        b = P.bank()
        pb = self.psb(b).rearrange("p (a b) -> p a b", a=4)
        ident = self.CF[:, 0:128]
        mms = [(lambda e, j=j: e.transpose(out=pb[:, j, :], in_=vs[:, j, :], identity=ident)) for j in range(4)]
        P.mm_group(mms, reads=["vs", "CF"], writes=[("PS", b)])
        P.op("dve", lambda e: e.tensor_copy(out=self.VEC, in_=self.psb(b)), reads=[("PS", b)], writes=["VEC"])
        P.op("dve", lambda e: e.tensor_copy(out=self.ONESB, in_=self.CF[:, 384:512]), reads=["CF"], writes=["ONESB"])
        P.op("dve", lambda e: e.tensor_copy(out=self.IDENTB, in_=self.CF[:, 0:128]), reads=["CF"], writes=["IDENTB"])

    def ph_load_x(self):
        P = self.P
        if not P.active:
            return
        P.barrier()
        x = self.din("x", [TOK, D])
        ident = self.CF[:, 0:128]
        xs = [self.av(i * 8192, [128, D], F32) for i in range(2)]
        dss = [self.ds("xs0"), self.ds("xs1")]
        for tt in range(8):
            s = tt % 2
            P.dma("sp", out=xs[s], in_=x[tt * 128:(tt + 1) * 128, :], writes=[("xs", s)], ds=dss[s])
            for g in range(4):
                b = P.bank()
                pb = self.psb(b).rearrange("p (a b) -> p a b", a=4)
                mms = [(lambda e, i=i, g=g, s=s, pb=pb: e.transpose(out=pb[:, i, :], in_=xs[s][:, (4 * g + i) * 128:(4 * g + i + 1) * 128], identity=ident))
                       for i in range(4)]
                P.mm_group(mms, reads=[("xs", s), "CF"], writes=[("PS", b)])
                dst = self.XT[:, 4 * g:4 * g + 4, tt * 128:(tt + 1) * 128]
                eng = "dve" if g % 2 == 0 else "act"
                if eng == "dve":
                    P.op("dve", lambda e, dst=dst, pb=pb: e.tensor_copy(out=dst, in_=pb), reads=[("PS", b)], writes=[("XTl", tt, g)])
                else:
                    P.op("act", lambda e, dst=dst, pb=pb: e.activation(out=dst, in_=pb, func=AF.Copy), reads=[("PS", b)], writes=[("XTl", tt, g)])

    def _stats(self, th, src_fn, nchunks, scale, rd_keys):
        P = self.P
        SQ = [self.av(self.o_sq + i * 4096, [128, 4, 512], BF16) for i in range(2)]
        RSTD = self.av(self.o_rstd, [128, 512], F32)
        b = P.bank()
        ps = self.psb(b)
        ngr = (nchunks + 3) // 4
        for q in range(ngr):
            n = min(4, nchunks - 4 * q)
            sq = SQ[q % 2]
            P.op("act", lambda e, q=q, n=n, sq=sq: e.activation(out=sq[:, 0:n, :], in_=src_fn(4 * q, n, th), func=AF.Square),
                 reads=rd_keys, writes=[("SQ", q % 2)])
            mms = [(lambda e, i=i, q=q, n=n, sq=sq: e.matmul(ps, lhsT=self.ONESB, rhs=sq[:, i, :], start=(q == 0 and i == 0), stop=(q == ngr - 1 and i == n - 1)))
                   for i in range(n)]
            P.mm_group(mms, reads=[("SQ", q % 2), "ONESB"], writes=[("PS", b)])
        P.op("act", lambda e: e.activation(out=RSTD, in_=ps, func=AF.Sqrt, bias=EPS, scale=scale), reads=[("PS", b)], writes=["RSTD"])
        P.op("dve", lambda e: e.reciprocal(out=RSTD, in_=RSTD), reads=["RSTD"], writes=["RSTD"])
        return RSTD

    def ph_norm(self, row):
        P = self.P
        if not P.active:
            return
        P.barrier()
        self._norm(row)

    def _norm(self, row):
        P = self.P
        XT, HT, VEC = self.XT, self.HT, self.VEC
        for th in range(2):
            ts = slice(th * 512, (th + 1) * 512)
            rd = [("XT", kc, th) for kc in range(KC)]
            RSTD = self._stats(th, lambda c0, n, th_: XT[:, c0:c0 + n, th_ * 512:(th_ + 1) * 512], KC, 1.0 / D, rd)
            pool_kc = [2, 5, 8, 11, 14] if USE_POOL else []
            dve_kc = [kc for kc in range(KC) if kc not in pool_kc]
            for kc in pool_kc:
                P.op("pool", lambda e, kc=kc, ts=ts: e.scalar_tensor_tensor(out=HT[:, kc, ts], in0=XT[:, kc, ts], scalar=VEC[:, row + kc:row + kc + 1],
                                                                             in1=RSTD, op0=ALU.mult, op1=ALU.mult),
                     reads=[("XT", kc, th), "RSTD", "VEC"], writes=[("HTp", th)])
            for kc in dve_kc:
                extra = [("HTp", th)] if (kc == dve_kc[-1] and pool_kc) else []
                P.op("dve", lambda e, kc=kc, ts=ts: e.scalar_tensor_tensor(out=HT[:, kc, ts], in0=XT[:, kc, ts], scalar=VEC[:, row + kc:row + kc + 1],
                                                                            in1=RSTD, op0=ALU.mult, op1=ALU.mult),
                     reads=[("XT", kc, th), "RSTD", "VEC"] + extra, writes=[("HT", th)])

    def ph_ffn(self, idx, norm_row):
        P = self.P
        if not P.active:
            return
        P.barrier()
        XT, HT = self.XT, self.HT
        Wg = self.din("ffn_w_gate", [4 * D, FF])[idx * D:(idx + 1) * D, :].rearrange("(kc p) n -> p kc n", p=128)
        Wu = self.din("ffn_w_up", [4 * D, FF])[idx * D:(idx + 1) * D, :].rearrange("(kc p) n -> p kc n", p=128)
        Wd = self.din("ffn_w_down", [4 * FF, D])[idx * FF:(idx + 1) * FF, :].rearrange("(c p) n -> p c n", p=128)
        tiles = []
        for g in range(NGRP):
            tiles.append([(Wg[:, :, g * 512:(g + 1) * 512], 16, 512)])
            tiles.append([(Wu[:, :, g * 512:(g + 1) * 512], 16, 512)])
            tiles.append([(Wd[:, 4 * g:4 * g + 4, :], 4, 2048)])
        self.ring_begin(5, tiles)
        self._norm(norm_row)
        o = 5 * 16384
        A = [self.av(o + i * 8192, [128, 4, TOK], BF16) for i in range(2)]
        TMP = [self.av(o + 16384 + i * 2048, [128, 512], F32) for i in range(2)]
        ti = 0
        for g in range(NGRP):
            wg, kg = self.ring_get()
            wu, ku = self.ring_get()
            Ab = A[g % 2]
            for c in range(4):
                for th in range(2):
                    ts = slice(th * 512, (th + 1) * 512)
                    bg = P.bank()
                    bu = P.bank()
                    pg, pu = self.psb(bg), self.psb(bu)
                    cs = slice(c * 128, (c + 1) * 128)
                    P.mm_group([(lambda e, kc=kc, pg=pg, wg=wg, cs=cs, ts=ts: e.matmul(pg, lhsT=wg[:, kc, cs], rhs=HT[:, kc, ts], start=(kc == 0), stop=(kc == KC - 1)))
                                for kc in range(KC)], reads=[kg, ("HT", th)], writes=[("PS", bg)])
                    P.mm_group([(lambda e, kc=kc, pu=pu, wu=wu, cs=cs, ts=ts: e.matmul(pu, lhsT=wu[:, kc, cs], rhs=HT[:, kc, ts], start=(kc == 0), stop=(kc == KC - 1)))
                                for kc in range(KC)], reads=[ku, ("HT", th)], writes=[("PS", bu)])
                    tmp = TMP[ti % 2]
                    tk = ("TMP", ti % 2)
                    ti += 1
                    P.op("act", lambda e, tmp=tmp, pg=pg: e.activation(out=tmp, in_=pg, func=AF.Silu), reads=[("PS", bg)], writes=[tk])
                    P.op("dve", lambda e, tmp=tmp, pu=pu, Ab=Ab, c=c, ts=ts: e.tensor_tensor(out=Ab[:, c, ts], in0=tmp, in1=pu, op=ALU.mult),
                         reads=[tk, ("PS", bu)], writes=[("A", g % 2, c, th)])
            self.ring_done(2)
            wd, kd = self.ring_get()
            for dmc in range(KC):
                ds_ = slice(dmc * 128, (dmc + 1) * 128)
                for th in range(2):
                    ts = slice(th * 512, (th + 1) * 512)
                    bo = P.bank()
                    po = self.psb(bo)
                    P.mm_group([(lambda e, c=c, po=po, wd=wd, ds_=ds_, ts=ts, Ab=Ab: e.matmul(po, lhsT=wd[:, c, ds_], rhs=Ab[:, c, ts], start=(c == 0), stop=(c == 3)))
                                for c in range(4)], reads=[kd] + [("A", g % 2, c, th) for c in range(4)], writes=[("PS", bo)])
                    P.op("dve", lambda e, po=po, dmc=dmc, ts=ts: e.scalar_tensor_tensor(out=XT[:, dmc, ts], in0=po, scalar=0.5, in1=XT[:, dmc, ts],
                                                                                          op0=ALU.mult, op1=ALU.add),
                         reads=[("PS", bo), ("XT", dmc, th)], writes=[("XT", dmc, th)])
            self.ring_done(1)

    def ph_pl(self, l, norm_row):
        P = self.P
        if not P.active:
            return
        P.barrier()
        XT, HT = self.XT, self.HT
        ident = self.CF[:, 0:128]
        Wg = self.din("pl_w_gate", [2 * D, D])[l * D:(l + 1) * D, :].rearrange("(kc p) n -> p kc n", p=128)
        Wp = self.din("pl_w_proj", [2 * 256, D])[l * 256:(l + 1) * 256, :].rearrange("(k p) n -> p k n", p=128)
        pin = self.din("p", [2 * TOK, 256])[l * TOK:(l + 1) * TOK, :]
        tiles = [[(Wp, 2, 2048)]] + [[(Wg[:, :, t * 512:(t + 1) * 512], 16, 512)] for t in range(4)]
        self.ring_begin(5, tiles)
        o = 5 * 16384
        PT = self.av(o, [128, 2, TOK], BF16)
        pst = self.av(o + 4096, [128, 8, 256], F32)
        TMP = [self.av(o + 12288 + i * 2048, [128, 512], F32) for i in range(2)]
        TM2 = [self.av(o + 16384 + i * 2048, [128, 512], F32) for i in range(2)]
        P.dma("sp", out=pst, in_=pin.rearrange("(tt p) c -> p tt c", p=128), writes=["pst"], ds=self.ds("pst"))
        for tt in range(8):
            b = P.bank()
            pb = self.psb(b).rearrange("p (a b) -> p a b", a=4)
            P.mm_group([(lambda e, k=k, tt=tt, pb=pb: e.transpose(out=pb[:, k, :], in_=pst[:, tt, k * 128:(k + 1) * 128], identity=ident)) for k in range(2)],
                       reads=["pst", "CF"], writes=[("PS", b)])
            P.op("dve", lambda e, tt=tt, pb=pb: e.tensor_copy(out=PT[:, :, tt * 128:(tt + 1) * 128], in_=pb[:, 0:2, :]), reads=[("PS", b)], writes=[("PT", tt // 4)])
        self._norm(norm_row)
        wp, kp = self.ring_get()
        ti = 0
        for t in range(4):
            w, kw = self.ring_get()
            for cc in range(4):
                dmc = 4 * t + cc
                cs = slice(cc * 128, (cc + 1) * 128)
                ds_ = slice(dmc * 128, (dmc + 1) * 128)
                for th in range(2):
                    ts = slice(th * 512, (th + 1) * 512)
                    bg = P.bank()
                    bp = P.bank()
                    pg, pp = self.psb(bg), self.psb(bp)
                    P.mm_group([(lambda e, kc=kc, pg=pg, w=w, cs=cs, ts=ts: e.matmul(pg, lhsT=w[:, kc, cs], rhs=HT[:, kc, ts], start=(kc == 0), stop=(kc == KC - 1)))
                                for kc in range(KC)], reads=[kw, ("HT", th)], writes=[("PS", bg)])
                    P.mm_group([(lambda e, k=k, pp=pp, ds_=ds_, ts=ts: e.matmul(pp, lhsT=wp[:, k, ds_], rhs=PT[:, k, ts], start=(k == 0), stop=(k == 1)))
                                for k in range(2)], reads=[kp, ("PT", th)], writes=[("PS", bp)])
                    tmp, tm2 = TMP[ti % 2], TM2[ti % 2]
                    k1, k2 = ("TMP", ti % 2), ("TM2", ti % 2)
                    ti += 1
                    P.op("act", lambda e, tmp=tmp, pg=pg: e.activation(out=tmp, in_=pg, func=AF.Sigmoid), reads=[("PS", bg)], writes=[k1])
                    P.op("dve", lambda e, tmp=tmp, tm2=tm2, pp=pp: e.tensor_tensor(out=tm2, in0=tmp, in1=pp, op=ALU.mult), reads=[k1, ("PS", bp)], writes=[k2])
                    P.op("dve", lambda e, tm2=tm2, dmc=dmc, ts=ts: e.tensor_tensor(out=XT[:, dmc, ts], in0=XT[:, dmc, ts], in1=tm2, op=ALU.add),
                         reads=[k2, ("XT", dmc, th)], writes=[("XT", dmc, th)])
            self.ring_done(1)

    def ph_final(self, normalize=True):
        P = self.P
        if not P.active:
            return
        P.barrier()
        XT, VEC = self.XT, self.VEC
        ident = self.CF[:, 0:128]
        out = self.dout("out", [TOK, D])
        if normalize:
            for th in range(2):
                ts = slice(th * 512, (th + 1) * 512)
                rd = [("XT", kc, th) for kc in range(KC)]
                RSTD = self._stats(th, lambda c0, n, th_: XT[:, c0:c0 + n, th_ * 512:(th_ + 1) * 512], KC, 1.0 / D, rd)
                for kc in range(KC):
                    P.op("dve", lambda e, kc=kc, ts=ts: e.scalar_tensor_tensor(out=XT[:, kc, ts], in0=XT[:, kc, ts], scalar=VEC[:, R_FIN + kc:R_FIN + kc + 1],
                                                                                in1=RSTD, op0=ALU.mult, op1=ALU.mult),
                         reads=[("XT", kc, th), "RSTD", "VEC"], writes=[("XT", kc, th)])
        osb = [self.av(i * 8192, [128, D], F32) for i in range(2)]
        dss = [self.ds("os0"), self.ds("os1")]
        for tt in range(8):
            s = tt % 2
            th = tt // 4
            for g in range(4):
                b = P.bank()
                pb = self.psb(b).rearrange("p (a b) -> p a b", a=4)
                P.mm_group([(lambda e, i=i, g=g, tt=tt, pb=pb: e.transpose(out=pb[:, i, :], in_=XT[:, 4 * g + i, tt * 128:(tt + 1) * 128], identity=ident))
                            for i in range(4)], reads=[("XT", 4 * g + i, th) for i in range(4)] + ["CF"], writes=[("PS", b)])
                dst = osb[s][:, g * 512:(g + 1) * 512]
                if g % 2 == 0:
                    P.op("dve", lambda e, dst=dst, b=b: e.tensor_copy(out=dst, in_=self.psb(b)), reads=[("PS", b)], writes=[("os", s, g)])
                else:
                    P.op("act", lambda e, dst=dst, b=b: e.activation(out=dst, in_=self.psb(b), func=AF.Copy), reads=[("PS", b)], writes=[("os", s, g)])
            P.dma("sp", out=out[tt * 128:(tt + 1) * 128, :], in_=osb[s], reads=[("os", s, g) for g in range(4)], writes=[("outd", tt)], ds=dss[s])

    def ph_dbg_ht(self):
        P = self.P
        P.barrier()
        for kc in range(KC):
            for th in range(2):
                ts = slice(th * 512, (th + 1) * 512)
                P.op("dve", lambda e, kc=kc, ts=ts: e.tensor_copy(out=self.XT[:, kc, ts], in_=self.HT[:, kc, ts]),
                     reads=[("HT", th)], writes=[("XT", kc, th)])

    def dbg_dump(self, name, ap, shape, dtype):
        if not os.environ.get("K_DUMP"):
            return
        t = self.dout("dbg_" + name, shape, dtype)
        self.P.wait_all("sp")
        self.P.dma("sp", out=t, in_=ap, writes=[("dbg", name)], ds=self.ds("dbg_" + name))

    def xt(self, name, shape, dtype, producer):
        if self.seg is None:
            return self.dscratch(name, shape, dtype)
        if producer:
            return self.dout(name + "_o", shape, dtype)
        return self.din(name + "_i", shape, dtype)

    def exchange(self, k):
        P = self.P
        if self.seg is None:
            self.fused_exchange(k)
        else:
            if P.active:
                self.dump_state()
        self.cur_seg = k + 1
        self._set_active()
        if self.seg is not None and P.active:
            self.restore_state()

    def _coll(self, name, shape, dt, rkeys, wkey):
        P = self.P
        if not P.active or self.seg is not None:
            return
        groups = [[0, 1], [2, 3], [4, 5], [6, 7]]
        src = self.dram[name]
        dst = self.dscratch(name + "_g", [2 * shape[0], shape[1]], dt)
        cs = self.ds("cc_" + name)
        P.nobar.add(cs)
        P.keep_keys.add(wkey)
        P._wait_deps("pool", rkeys, [wkey])
        cs.count += 1
        P.ops["pool"].append(lambda e, src=src, dst=dst, sem=cs.sem: e.collective_compute(
            "AllGather", ALU.bypass, replica_groups=groups, ins=[src], outs=[dst]).then_inc(sem, 1))
        P._record(cs, cs.count, rkeys, [wkey])

    def fused_exchange(self, k):
        if k == 0:
            self._coll("ex1_u", [128, 240], BF16, ["d_ex1_u"], "d_ex1_ug")

    def halo_src(self, name, shape, dtype):
        if self.seg is None:
            return self.dram[name + "_g"][0:shape[0]]
        return self.din(name + "_h", shape, dtype)

    def _m0_layout(self):
        R0 = 2 * 16384
        L = {}
        L["O"] = self.av(R0, [128, 8, TOK], BF16)
        L["QH"] = self.av(R0 + 16384, [128, 4, TOK], BF16)
        L["SE"] = self.av(R0 + 24576, [128, 4, 256], F32)
        T0 = R0 + 28672
        L["T0"] = T0
        L["U"] = self.av(T0, [128, 8, 1054], BF16)
        L["SR"] = self.av(T0 + 17664, [128, 8, TOK], BF16)
        L["X0"] = T0 + 17664 + 16384
        return L

    def ph_mixer0_a(self, norm_row):
        P = self.P
        if not P.active:
            return
        P.barrier()
        HT, CF = self.HT, self.CF
        MASK = CF[:, 128:256]
        LM = CF[:, 256:384]
        Win = self.din("ab_w_in", [D, ABIN]).rearrange("(kc p) n -> p kc n", p=128)
        gw_d = self.din("gwaug", [17, 512])
        L = self._m0_layout()
        O, QH, SE = L["O"], L["QH"], L["SE"]
        T0 = L["T0"]
        GZ = self.av(T0, [32, TOK], BF16)
        GW = self.av(T0 + 2048, [32, 512], BF16)
        LA = self.av(T0 + 3072, [128, 8, 128], F32)
        ENB = self.av(T0 + 7168, [128, 8, 128], F32)
        EBT = self.av(T0 + 11264, [128, TOK], F32)
        ENBT = self.av(T0 + 15360, [128, TOK], F32)
        KT = self.av(T0 + 19456, [128, TOK], BF16)
        KTM = self.av(T0 + 21504, [128, 8, 128], BF16)
        VH = self.av(T0 + 23552, [128, 8, 256], BF16)
        S32 = self.av(T0 + 27648, [128, 256], F32)
        SBF = self.av(T0 + 28672, [128, 256], BF16)
        TT = self.av(T0 + 29184, [128, 256], F32)
        AM = [self.av(T0 + 30208 + i * 256, [128, 128], BF16) for i in range(2)]
        EBL = self.av(T0 + 30720, [128, 16], F32)
        EC = self.av(T0 + 30784, [128, 16], F32)
        SBFR = [self.av(T0 + 30848 + i * 512, [128, 256], BF16) for i in range(8)]
        TTR = [TT, self.av(T0 + 34944, [128, 256], F32)]
        tiles = [[(Win[:, :, 3072:3088], 16, 16)]]
        for h in range(4):
            tiles.append([(Win[:, :, h * 128:(h + 1) * 128], 16, 128),
                          (Win[:, :, 512 + h * 128:512 + (h + 1) * 128], 16, 128),
                          (Win[:, :, 1024 + h * 256:1024 + (h + 1) * 256], 16, 256)])
        tiles.append([(Win[:, :, 2048:2560], 16, 512)])
        tiles.append([(Win[:, :, 2560:3072], 16, 512)])
        for cb in range(2):
            tiles.append([(Win[:, :, 3088 + cb * 512:3088 + (cb + 1) * 512], 16, 512)])
            tiles.append([(Win[:, :, 4112 + cb * 512:4112 + (cb + 1) * 512], 16, 512)])
        self.ring_begin(2, tiles)
        self._norm(norm_row)
        qscale = 128.0 ** -0.5
        P.op("dve", lambda e: e.memset(GZ, 1.0), writes=["GZ"])
        P.op("dve", lambda e: e.memset(GW, 0.0), writes=["GW"])
        P.dma("pool", out=GW[0:17, :], in_=gw_d, reads=[], writes=["GW"], ds=self.ds("gw"))
        wz, kz = self.ring_get()
        for th in range(2):
            ts = slice(th * 512, (th + 1) * 512)
            b = P.bank()
            ps = self.psb(b)
            P.mm_group([(lambda e, kc=kc, ps=ps, ts=ts: e.matmul(ps[0:16, :], lhsT=wz[:, kc, 0:16], rhs=HT[:, kc, ts], start=(kc == 0), stop=(kc == KC - 1)))
                        for kc in range(KC)], reads=[kz, ("HT", th)], writes=[("PS", b)])
            P.op("dve", lambda e, ps=ps, ts=ts: e.tensor_copy(out=GZ[0:16, ts], in_=ps[0:16, :]), reads=[("PS", b), "GZ"], writes=["GZ"])
        self.ring_done(1)
        DBG = int(os.environ.get("K_DBG", "99"))
        if DBG == 0:
            return
        P.barrier()
        KTs = [KT, self.av(self.o_sq, [128, TOK], BF16)]
        KTMs = [KTM, self.av(self.o_sq + 2048, [128, 8, 128], BF16)]
        VHs = [VH, self.av(self.o_sq + 4096, [128, 8, 256], BF16)]
        EBLs = [EBL, self.av(self.o_rstd, [128, 16], F32)]

        def front(h):
            bs = h % 2
            KT_, KTM_, VH_, EBL_ = KTs[bs], KTMs[bs], VHs[bs], EBLs[bs]
            kKT, kKTM, kVH, kEBL = ("KT", bs), ("KTM", bs), ("VH", bs), ("EBL", bs)
            w, kw = self.ring_get()
            hs = slice(h * 128, (h + 1) * 128)
            for q in range(2):
                b = P.bank()
                pb = self.psb(b).rearrange("p (a b) -> p a b", a=4)
                P.mm_group([(lambda e, i=i, q=q, pb=pb: e.matmul(pb[:, i, :], lhsT=GZ[0:17, (4 * q + i) * 128:(4 * q + i + 1) * 128], rhs=GW[0:17, hs], start=True, stop=True))
                            for i in range(4)], reads=["GZ", "GW"], writes=[("PS", b)])
                P.op("act", lambda e, q=q, pb=pb: e.activation(out=LA[:, 4 * q:4 * q + 4, :], in_=pb, func=AF.Exp, scale=-1.0), reads=[("PS", b)], writes=["LA"])
                P.op("act", lambda e, q=q: e.activation(out=LA[:, 4 * q:4 * q + 4, :], in_=LA[:, 4 * q:4 * q + 4, :], func=AF.Ln, bias=1.0), reads=["LA"], writes=["LA"])
                yield
            for q in range(2):
                b = P.bank()
                pb = self.psb(b).rearrange("p (a b) -> p a b", a=4)
                P.mm_group([(lambda e, i=i, q=q, pb=pb: e.matmul(pb[:, i, :], lhsT=LM, rhs=LA[:, 4 * q + i, :], start=True, stop=True)) for i in range(4)],
                           reads=["LA", "CF"], writes=[("PS", b)])
                P.op("act", lambda e, q=q, pb=pb: e.activation(out=ENB[:, 4 * q:4 * q + 4, :], in_=pb, func=AF.Exp, scale=-1.0), reads=[("PS", b)], writes=["ENB"])
                b2 = P.bank()
                pb2 = self.psb(b2).rearrange("p (a b) -> p a b", a=4)
                P.mm_group([(lambda e, i=i, q=q, pb2=pb2: e.matmul(pb2[:, i, :], lhsT=LA[:, 4 * q + i, :], rhs=LM, start=True, stop=True)) for i in range(4)],
                           reads=["LA", "CF"], writes=[("PS", b2)])
                P.op("act", lambda e, q=q, b2=b2: e.activation(out=EBT[:, q * 512:(q + 1) * 512], in_=self.psb(b2), func=AF.Exp), reads=[("PS", b2)], writes=["EBT"])
                yield
            P.op("dve", lambda e: e.reciprocal(out=ENBT, in_=EBT), reads=["EBT"], writes=["ENBT"])
            P.op("dve", lambda e: e.tensor_copy(out=EBL_, in_=EBT.rearrange("p (j c) -> p j c", c=64)[:, :, 63]), reads=["EBT"], writes=[kEBL])
            yield
            for th in range(2):
                ts = slice(th * 512, (th + 1) * 512)
                b = P.bank()
                ps = self.psb(b)
                P.mm_group([(lambda e, kc=kc, ps=ps, ts=ts: e.matmul(ps, lhsT=w[:, kc, 0:128], rhs=HT[:, kc, ts], start=(kc == 0), stop=(kc == KC - 1)))
                            for kc in range(KC)], reads=[kw, ("HT", th)], writes=[("PS", b)])
                P.op("dve", lambda e, ps=ps, ts=ts: e.scalar_tensor_tensor(out=QH[:, h, ts], in0=ps, scalar=qscale, in1=EBT[:, ts], op0=ALU.mult, op1=ALU.mult),
                     reads=[("PS", b), "EBT"], writes=[("QH", h)])
                yield
                b = P.bank()
                ps = self.psb(b)
                P.mm_group([(lambda e, kc=kc, ps=ps, ts=ts: e.matmul(ps, lhsT=w[:, kc, 128:256], rhs=HT[:, kc, ts], start=(kc == 0), stop=(kc == KC - 1)))
                            for kc in range(KC)], reads=[kw, ("HT", th)], writes=[("PS", b)])
                P.op("dve", lambda e, ps=ps, ts=ts: e.tensor_tensor(out=KT_[:, ts], in0=ps, in1=ENBT[:, ts], op=ALU.mult), reads=[("PS", b), "ENBT"], writes=[kKT])
                yield
            for tt in range(8):
                tsl = slice(tt * 128, (tt + 1) * 128)
                b = P.bank()
                ps = self.psb(b)
                P.mm_group([(lambda e, kc=kc, ps=ps, tsl=tsl: e.matmul(ps[:, 0:384], lhsT=HT[:, kc, tsl], rhs=w[:, kc, 128:512], start=(kc == 0), stop=(kc == KC - 1)))
                            for kc in range(KC)], reads=[kw, ("HT", tt // 4)], writes=[("PS", b)])
                P.op("dve", lambda e, ps=ps, tt=tt: e.tensor_tensor(out=KTM_[:, tt, :], in0=ps[:, 0:128], in1=ENB[:, tt, :], op=ALU.mult), reads=[("PS", b), "ENB"], writes=[kKTM, ("PSR", b)])
                P.op("act", lambda e, ps=ps, tt=tt: e.activation(out=VH_[:, tt, :], in_=ps[:, 128:384], func=AF.Copy), reads=[("PS", b), ("PSR", b)], writes=[kVH])
                yield
            self.ring_done(1)

        def back(h):
            bs = h % 2
            KT_, KTM_, VH_, EBL_ = KTs[bs], KTMs[bs], VHs[bs], EBLs[bs]
            kKT, kKTM, kVH, kEBL = ("KT", bs), ("KTM", bs), ("VH", bs), ("EBL", bs)
            P.op("dve", lambda e: e.memset(S32, 0.0), writes=["S32"])

            def finish_tile(tt, po, bo):
                tsl = slice(tt * 128, (tt + 1) * 128)
                for hh in range(2):
                    j = 2 * tt + hh
                    if j == 0:
                        continue
                    sb = SBFR[(j - 1) % 8]
                    sk = ("SBF", (j - 1) % 8)
                    cs = slice(j * 64, (j + 1) * 64)
                    P.mm_group([(lambda e, vc=vc, hh=hh, cs=cs, sb=sb: e.matmul(po[:, vc, hh * 64:(hh + 1) * 64], lhsT=sb[:, vc * 128:(vc + 1) * 128], rhs=QH[:, h, cs],
                                                                                   start=False, stop=True, skip_group_check=True))
                                for vc in range(2)], reads=[sk, ("QH", h)], writes=[("PS", bo)])
                P.op("act", lambda e: e.activation(out=O[:, 2 * h:2 * h + 2, tsl], in_=po[:, 0:2, :], func=AF.Copy), reads=[("PS", bo)], writes=[("O", h)])

            pending = None
            for tt in range(8):
                tsl = slice(tt * 128, (tt + 1) * 128)
                ba = P.bank()
                pa = self.psb(ba)[:, 0:128]
                P.mm_group([lambda e, pa=pa, tsl=tsl: e.matmul(pa, lhsT=KT_[:, tsl], rhs=QH[:, h, tsl], start=True, stop=True)],
                           reads=[kKT, ("QH", h)], writes=[("PS", ba)])
                am = AM[tt % 2]
                P.op("dve", lambda e, pa=pa, am=am: e.tensor_tensor(out=am, in0=pa, in1=MASK, op=ALU.mult), reads=[("PS", ba), "CF"], writes=[("AM", tt % 2)])
                bo = 6 + (tt % 2)
                po = self.psb(bo).rearrange("p (a b) -> p a b", a=4)
                P.mm_group([(lambda e, vc=vc, po=po, tt=tt, am=am: e.matmul(po[:, vc, :], lhsT=VH_[:, tt, vc * 128:(vc + 1) * 128], rhs=am, start=(vc == 0), stop=False, skip_group_check=True))
                            for vc in range(2)], reads=[kVH, ("AM", tt % 2)], writes=[("PS", bo)])
                for hh in range(2):
                    j = 2 * tt + hh
                    bk = P.bank()
                    pkv = self.psb(bk)[:, 0:256]
                    P.mm_group([lambda e, pkv=pkv, hh=hh, tt=tt: e.matmul(pkv, lhsT=KTM_[hh * 64:(hh + 1) * 64, tt, :], rhs=VH_[hh * 64:(hh + 1) * 64, tt, :], start=True, stop=True)],
                               reads=[kKTM, kVH], writes=[("PS", bk)])
                    ttb = TTR[j % 2]
                    tk = ("TT", j % 2)
                    P.op("dve", lambda e, pkv=pkv, j=j, ttb=ttb: e.tensor_scalar(out=ttb, in0=pkv, scalar1=EBL_[:, j:j + 1], scalar2=None, op0=ALU.mult),
                         reads=[("PS", bk), kEBL], writes=[tk])
                    P.op("dve", lambda e, j=j, ttb=ttb: e.scalar_tensor_tensor(out=S32, in0=S32, scalar=EBL_[:, j:j + 1], in1=ttb, op0=ALU.mult, op1=ALU.add),
                         reads=["S32", tk, kEBL], writes=["S32"])
                    if j < 15:
                        sb = SBFR[j % 8]
                        P.op("act", lambda e, sb=sb: e.activation(out=sb, in_=S32, func=AF.Copy), reads=["S32"], writes=[("SBF", j % 8)])
                if pending is not None:
                    finish_tile(*pending)
                pending = (tt, po, bo)
                yield
            finish_tile(*pending)
            P.op("act", lambda e: e.activation(out=SE[:, h, :], in_=S32, func=AF.Copy), reads=["S32"], writes=["SE"])
            P.op("dve", lambda e: e.memset(EC[:, 0:1], 1.0), writes=["EC"])
            for j in range(1, 16):
                P.op("dve", lambda e, j=j: e.tensor_tensor(out=EC[:, j:j + 1], in0=EC[:, j - 1:j], in1=EBL_[:, j - 1:j], op=ALU.mult), reads=["EC", kEBL], writes=["EC"])
            yield
            for j in range(1, 16):
                cs = slice(j * 64, (j + 1) * 64)
                P.op("dve", lambda e, j=j, cs=cs: e.tensor_scalar(out=QH[:, h, cs], in0=QH[:, h, cs], scalar1=EC[:, j:j + 1], scalar2=None, op0=ALU.mult),
                     reads=["EC", ("QH", h)], writes=[("QH", h)])

        def drain(g):
            for _ in g:
                pass

        P.bank_mod = 6
        drain(front(0))
        for h in range(4):
            g_l = back(h)
            g_f = front(h + 1) if h + 1 < 4 else iter(())
            l_live, f_live = True, True
            while l_live or f_live:
                for _ in range(2):
                    if f_live:
                        try:
                            next(g_f)
                        except StopIteration:
                            f_live = False
                if l_live:
                    try:
                        next(g_l)
                    except StopIteration:
                        l_live = False
        P.bank_mod = 8
        if DBG == 5:
            return
        ex_s = self.xt("ex1_s", [128, 1024], F32, True)
        P.dma("sp", out=ex_s, in_=SE.rearrange("p a b -> p (a b)"), reads=["SE"], writes=["d_ex1_s"], ds=self.ds("ex1s"))
        self._coll("ex1_s", [128, 1024], F32, ["d_ex1_s"], "d_ex1_sg")
        P.barrier()
        self.dbg_dump("LA", LA.rearrange("p a b -> p (a b)"), [128, 1024], F32)
        self.dbg_dump("ENB", ENB.rearrange("p a b -> p (a b)"), [128, 1024], F32)
        self.dbg_dump("EBT", EBT, [128, 1024], F32)
        self.dbg_dump("KT", KT, [128, 1024], BF16)
        self.dbg_dump("KTM", KTM.rearrange("p a b -> p (a b)"), [128, 1024], BF16)
        self.dbg_dump("VH", VH.rearrange("p a b -> p (a b)"), [128, 2048], BF16)
        self.dbg_dump("O", O.rearrange("p a b -> p (a b)"), [128, 8192], BF16)
        self.dbg_dump("QH", QH.rearrange("p a b -> p (a b)"), [128, 4096], BF16)
        self.dbg_dump("SE", SE.rearrange("p a b -> p (a b)"), [128, 1024], F32)
        self.dbg_dump("GZ", GZ, [32, 1024], BF16)
        self.dbg_dump("GW", GW, [32, 512], BF16)
        self.dbg_dump("EBL", EBL, [128, 16], F32)
        P.barrier()
        U, SR = L["U"], L["SR"]
        TMP = [self.av(L["X0"] + i * 2048, [128, 512], F32) for i in range(2)]
        ti = 0
        for t2 in range(2):
            w, kw = self.ring_get()
            for c in range(4):
                cs = slice(c * 128, (c + 1) * 128)
                for th in range(2):
                    ts = slice(th * 512, (th + 1) * 512)
                    b = P.bank()
                    ps = self.psb(b)
                    P.mm_group([(lambda e, kc=kc, ps=ps, cs=cs, ts=ts, w=w: e.matmul(ps, lhsT=w[:, kc, cs], rhs=HT[:, kc, ts], start=(kc == 0), stop=(kc == KC - 1)))
                                for kc in range(KC)], reads=[kw, ("HT", th)], writes=[("PS", b)])
                    P.op("act", lambda e, ps=ps, t2=t2, c=c, ts=ts: e.activation(out=SR[:, t2 * 4 + c, ts], in_=ps, func=AF.Silu), reads=[("PS", b)], writes=[("SR", t2 * 4 + c)])
            self.ring_done(1)
        for cb in range(2):
            wa, ka = self.ring_get()
            wb, kb = self.ring_get()
            for c in range(4):
                cs = slice(c * 128, (c + 1) * 128)
                for th in range(2):
                    ts = slice(th * 512, (th + 1) * 512)
                    b1 = P.bank()
                    b2 = P.bank()
                    p1, p2 = self.psb(b1), self.psb(b2)
                    P.mm_group([(lambda e, kc=kc, p1=p1, cs=cs, ts=ts, wa=wa: e.matmul(p1, lhsT=wa[:, kc, cs], rhs=HT[:, kc, ts], start=(kc == 0), stop=(kc == KC - 1)))
                                for kc in range(KC)], reads=[ka, ("HT", th)], writes=[("PS", b1)])
                    P.mm_group([(lambda e, kc=kc, p2=p2, cs=cs, ts=ts, wb=wb: e.matmul(p2, lhsT=wb[:, kc, cs], rhs=HT[:, kc, ts], start=(kc == 0), stop=(kc == KC - 1)))
                                for kc in range(KC)], reads=[kb, ("HT", th)], writes=[("PS", b2)])
                    tmp = TMP[ti % 2]
                    tk = ("TMP", ti % 2)
                    ti += 1
                    P.op("act", lambda e, tmp=tmp, p2=p2: e.activation(out=tmp, in_=p2, func=AF.Sigmoid), reads=[("PS", b2)], writes=[tk])
                    P.op("dve", lambda e, tmp=tmp, p1=p1, cb=cb, c=c, th=th: e.tensor_tensor(out=U[:, cb * 4 + c, 30 + th * 512:30 + (th + 1) * 512], in0=tmp, in1=p1, op=ALU.mult),
                         reads=[tk, ("PS", b1)], writes=[("U", cb * 4 + c)])
            self.ring_done(2)
        ex_u = self.xt("ex1_u", [128, 240], BF16, True)
        P.dma("sp", out=ex_u.rearrange("p (c w) -> p c w", c=8), in_=U[:, :, 1024:1054], reads=[("U", c) for c in range(8)], writes=["d_ex1_u"], ds=self.ds("ex1u"))

    def _outproj_prefetch(self, wname):
        Wo = self.din(wname, [D, D]).rearrange("(kc p) n -> p kc n", p=128)
        self.ring_begin(2, [[(Wo[:, :, t * 512:(t + 1) * 512], 16, 512)] for t in range(4)])

    def _outproj(self, wname, prefetched=False):
        P = self.P
        XT, HT = self.XT, self.HT
        if not prefetched:
            self._outproj_prefetch(wname)
        for t in range(4):
            w, kw = self.ring_get()
            for cc in range(4):
                dmc = 4 * t + cc
                cs = slice(cc * 128, (cc + 1) * 128)
                for th in range(2):
                    ts = slice(th * 512, (th + 1) * 512)
                    b = P.bank()
                    ps = self.psb(b)
                    P.mm_group([(lambda e, kc=kc, ps=ps, cs=cs, ts=ts, w=w: e.matmul(ps, lhsT=w[:, kc, cs], rhs=HT[:, kc, ts], start=(kc == 0), stop=(kc == KC - 1)))
                                for kc in range(KC)], reads=[kw], writes=[("PS", b)])
                    P.op("dve", lambda e, ps=ps, dmc=dmc, ts=ts: e.tensor_tensor(out=XT[:, dmc, ts], in0=XT[:, dmc, ts], in1=ps, op=ALU.add),
                         reads=[("PS", b), ("XT", dmc, th)], writes=[("XT", dmc, th)])
            self.ring_done(1)

    def ph_mixer0_b(self):
        P = self.P
        if not P.active:
            return
        P.barrier()
        HT, VEC, FL = self.HT, self.VEC, self.FL
        L = self._m0_layout()
        O, QH, SE, U, SR = L["O"], L["QH"], L["SE"], L["U"], L["SR"]
        X0 = L["X0"]
        SIN = SE.rearrange("p a b -> p (a b)")
        SINB = self.av(X0, [128, 4, 256], BF16)
        UT = self.av(X0 + 2048, [128, 8, 30], BF16)
        OT = self.av(0, [128, 2, 512], F32)
        TM = [self.av(4096 + i * 2048, [128, 512], F32) for i in range(2)]
        ACC = [self.av(8192 + i * 4096, [128, TOK], F32) for i in range(2)]
        QB = 2 * 16384 + 16384
        MEAN = self.av(QB, [128, 512], F32)
        M2 = self.av(QB + 2048, [128, 512], F32)
        RS2 = self.av(QB + 4096, [128, 512], F32)
        T1 = [self.av(QB + 6144 + i * 2048, [128, 512], F32) for i in range(2)]
        SQL = [self.av(QB + 10240 + i * 1024, [128, 512], BF16) for i in range(2)]
        xs = self.halo_src("ex1_s", [128, 1024], F32)
        xu = self.halo_src("ex1_u", [128, 240], BF16)
        P.dma("sp", out=SIN, in_=xs, reads=["d_ex1_sg"], writes=["SIN"], ds=self.ds("sin"))
        P.dma("sp", out=UT, in_=xu.rearrange("p (c w) -> p c w", c=8), reads=["d_ex1_ug"], writes=["UT"], ds=self.ds("ut"))
        P.op("dve", lambda e: e.tensor_scalar(out=SINB.rearrange("p a b -> p (a b)"), in0=SIN, scalar1=FL[:, 0:1], scalar2=None, op0=ALU.mult), reads=["SIN", "FL"], writes=["SINB"])
        P.op("dve", lambda e: e.tensor_scalar(out=U[:, :, 0:30], in0=UT, scalar1=FL[:, 0:1], scalar2=None, op0=ALU.mult), reads=["UT", "FL"], writes=["UH"])
        ti = 0
        for h in range(4):
            for th in range(2):
                ts = slice(th * 512, (th + 1) * 512)
                for vc in range(2):
                    b = P.bank()
                    ps = self.psb(b)
                    P.mm_group([lambda e, ps=ps, vc=vc, ts=ts, h=h: e.matmul(ps, lhsT=SINB[:, h, vc * 128:(vc + 1) * 128], rhs=QH[:, h, ts], start=True, stop=True)],
                               reads=["SINB"], writes=[("PS", b)])
                    P.op("dve", lambda e, ps=ps, vc=vc, ts=ts, h=h: e.tensor_tensor(out=OT[:, vc, :], in0=ps, in1=O[:, 2 * h + vc, ts], op=ALU.add),
                         reads=[("PS", b)], writes=[("OT", vc)])
                RSTD = self._stats(th, lambda c0, n, th_: OT[:, c0:c0 + n, :], 2, 1.0 / 256, [("OT", 0), ("OT", 1)])
                for vc in range(2):
                    tm = TM[ti % 2]
                    tk = ("TM", ti % 2)
                    ti += 1
                    P.op("dve", lambda e, tm=tm, vc=vc: e.scalar_tensor_tensor(out=tm, in0=OT[:, vc, :], scalar=VEC[:, R_GLN + vc:R_GLN + vc + 1], in1=RSTD, op0=ALU.mult, op1=ALU.mult),
                         reads=[("OT", vc), "RSTD"], writes=[tk])
                    P.op("dve", lambda e, tm=tm, vc=vc, ts=ts, h=h: e.tensor_tensor(out=HT[:, 2 * h + vc, ts], in0=tm, in1=SR[:, 2 * h + vc, ts], op=ALU.mult),
                         reads=[tk], writes=[("CC", 2 * h + vc, th)])
        P.barrier()
        YB = self.av(2 * 16384, [128, 8, TOK], BF16)
        DG = [self.av(L["T0"] + 17664 + i * 8192, [128, 31, 128], BF16) for i in range(2)]
        self._outproj_prefetch("ab_w_out")

        def conv_chunk(c):
            dg = DG[c % 2]
            kd, ka = ("DGd", c % 2), ("DGa", c % 2)
            for wi in range(31):
                col = VEC[:, R_DW + wi * 8 + c:R_DW + wi * 8 + c + 1]
                if wi % 2 == 0:
                    P.op("dve", lambda e, wi=wi, col=col: e.tensor_scalar(out=dg[:, wi, :], in0=self.IDENTB, scalar1=col, scalar2=None, op0=ALU.mult),
                         reads=["IDENTB"], writes=[kd])
                else:
                    P.op("act", lambda e, wi=wi, col=col: e.activation(out=dg[:, wi, :], in_=self.IDENTB, func=AF.Identity, scale=col),
                         reads=["IDENTB"], writes=[ka])
            for th in range(2):
                b = P.bank()
                ps = self.psb(b)
                P.mm_group([(lambda e, wi=wi, th=th, ps=ps: e.matmul(ps, lhsT=dg[:, wi, :], rhs=U[:, c, wi + th * 512:wi + th * 512 + 512], start=(wi == 0), stop=(wi == 30)))
                            for wi in range(31)], reads=[kd, ka, "UH"], writes=[("PS", b)])
                P.op("act", lambda e, th=th, ps=ps: e.activation(out=YB[:, c, th * 512:(th + 1) * 512], in_=ps, func=AF.Identity, bias=VEC[:, R_CB + c:R_CB + c + 1]),
                     reads=[("PS", b)], writes=[("YB", c, th)])

        for c in range(8):
            conv_chunk(c)
        ti = 0
        for th in range(2):
            ts = slice(th * 512, (th + 1) * 512)
            b1 = P.bank()
            b2 = P.bank()
            p1, p2 = self.psb(b1), self.psb(b2)
            for c in range(8):
                sq = SQL[c % 2]
                P.op("act", lambda e, sq=sq, c=c, ts=ts: e.activation(out=sq, in_=YB[:, c, ts], func=AF.Square), reads=[("YB", c, th)], writes=[("SQL", c % 2)])
                P.mm_group([lambda e, c=c, ts=ts, p1=p1: e.matmul(p1, lhsT=self.ONESB, rhs=YB[:, c, ts], start=(c == 0), stop=(c == 7), skip_group_check=True)],
                           reads=[("YB", c, th)], writes=[("PS", b1)])
                P.mm_group([lambda e, c=c, sq=sq, p2=p2: e.matmul(p2, lhsT=self.ONESB, rhs=sq, start=(c == 0), stop=(c == 7), skip_group_check=True)],
                           reads=[("SQL", c % 2)], writes=[("PS", b2)])
            P.op("act", lambda e, p1=p1: e.activation(out=MEAN, in_=p1, func=AF.Identity, scale=1.0 / 1024), reads=[("PS", b1)], writes=["MEAN"])
            P.op("dve", lambda e: e.tensor_tensor(out=M2, in0=MEAN, in1=MEAN, op=ALU.mult), reads=["MEAN"], writes=["M2"])
            P.op("dve", lambda e, p2=p2: e.scalar_tensor_tensor(out=M2, in0=p2, scalar=1.0 / 1024, in1=M2, op0=ALU.mult, op1=ALU.subtract), reads=[("PS", b2), "M2"], writes=["M2"])
            P.op("act", lambda e: e.activation(out=RS2, in_=M2, func=AF.Sqrt, bias=EPS, scale=1.0), reads=["M2"], writes=["RS2"])
            P.op("dve", lambda e: e.reciprocal(out=RS2, in_=RS2), reads=["RS2"], writes=["RS2"])
            for c in range(8):
                t1 = T1[ti % 2]
                tk = ("T1", ti % 2)
                ti += 1
                P.op("dve", lambda e, t1=t1, c=c, ts=ts: e.tensor_tensor(out=t1, in0=YB[:, c, ts], in1=MEAN, op=ALU.subtract), reads=[("YB", c, th), "MEAN"], writes=[tk])
                P.op("dve", lambda e, t1=t1: e.tensor_tensor(out=t1, in0=t1, in1=RS2, op=ALU.mult), reads=[tk, "RS2"], writes=[tk])
                P.op("act", lambda e, t1=t1, c=c, ts=ts: e.activation(out=HT[:, 8 + c, ts], in_=t1, func=AF.Silu, bias=VEC[:, R_LNB + c:R_LNB + c + 1], scale=VEC[:, R_LNG + c:R_LNG + c + 1]),
                     reads=[tk], writes=[("CC", 8 + c, th)])
        P.barrier()
        z = os.environ.get("K_ZERO", "")
        if z == "A":
            P.op("dve", lambda e: e.memset(HT[:, 0:8, :], 0.0), writes=["zz"])
        if z == "B":
            P.op("dve", lambda e: e.memset(HT[:, 8:16, :], 0.0), writes=["zz"])
        P.barrier()
        self._outproj("ab_w_out", prefetched=True)

    def ph_attn_a(self, norm_row):
        P = self.P
        if not P.active:
            return
        P.barrier()
        HT = self.HT
        Wqkv = self.din("att_w_qkv", [D, 3 * D]).rearrange("(kc p) n -> p kc n", p=128)
        self.ring_begin(2, [[(Wqkv[:, :, t * 512:(t + 1) * 512], 16, 512)] for t in (4, 5, 6, 7, 8, 9, 10, 11, 0, 1, 2, 3)])
        self._norm(norm_row)
        R0 = 2 * 16384
        QT = self.av(R0, [128, 16, TOK], BF16)
        KS = [self.av(R0 + 32768 + i * 2048, [128, TOK], BF16) for i in range(2)]
        VS = [self.av(R0 + 36864 + i * 1024, [128, 512], BF16) for i in range(4)]
        kT_own = self.xt("kT_own", [D, TOK], BF16, True)
        v_own = self.xt("v_own", [TOK, D], BF16, True)
        kT_tail = self.xt("kT_tail", [D, 512], BF16, True)
        v_tail = self.xt("v_tail", [512, D], BF16, True)

        def evac(eng, dst, ps, b, wkeys):
            if eng == "dve":
                P.op("dve", lambda e: e.tensor_copy(out=dst, in_=ps), reads=[("PS", b)], writes=wkeys)
            else:
                P.op("act", lambda e: e.activation(out=dst, in_=ps, func=AF.Copy), reads=[("PS", b)], writes=wkeys)

        def proj_fm(w, kw, c, th, dst, eng, wkeys):
            ts = slice(th * 512, (th + 1) * 512)
            cs = slice(c * 128, (c + 1) * 128)
            b = P.bank()
            ps = self.psb(b)
            P.mm_group([(lambda e, kc=kc: e.matmul(ps, lhsT=w[:, kc, cs], rhs=HT[:, kc, ts], start=(kc == 0), stop=(kc == KC - 1))) for kc in range(KC)],
                       reads=[kw, ("HT", th)], writes=[("PS", b)])
            evac(eng, dst, ps, b, wkeys)

        for t in range(4):
            w, kw = self.ring_get()
            for c in range(4):
                h = 4 * t + c
                ks = KS[h % 2]
                for th in range(2):
                    proj_fm(w, kw, c, th, ks[:, th * 512:(th + 1) * 512], "dve" if h % 2 == 0 else "act", [("KS", h % 2)])
                P.dma("sp", out=kT_own[h * 128:(h + 1) * 128, :], in_=ks, reads=[("KS", h % 2)], writes=[("d_kTo", h)], ds=self.ds("ks%d" % (h % 2)))
                P.dma("sp", out=kT_tail[h * 128:(h + 1) * 128, :], in_=ks[:, 512:1024], reads=[("KS", h % 2)], writes=[("d_kTt", h)], ds=self.ds("ks%d" % (h % 2)))
            self.ring_done(1)
        self._coll("kT_tail", [D, 512], BF16, [("d_kTt", h) for h in range(16)], "d_halo_k")
        vi = 0
        for t in range(4):
            w, kw = self.ring_get()
            for tt in range(8):
                tsl = slice(tt * 128, (tt + 1) * 128)
                b = P.bank()
                ps = self.psb(b)

                def vgroup(w=w, tsl=tsl, ps=ps):
                    return [(lambda e, kc=kc: e.matmul(ps, lhsT=HT[:, kc, tsl], rhs=w[:, kc, :], start=(kc == 0), stop=(kc == KC - 1))) for kc in range(KC)]
                P.mm_group(vgroup(), reads=[kw, ("HT", tt // 4)], writes=[("PS", b)])
                vs = VS[vi % 4]
                vk = ("VS", vi % 4)
                evac("dve" if vi % 2 == 0 else "act", vs, ps, b, [vk])
                P.dma("sp", out=v_own[tsl, t * 512:(t + 1) * 512], in_=vs, reads=[vk], writes=[("d_vo", t, tt)], ds=self.ds("vs%d" % (vi % 4)))
                if tt >= 4:
                    P.dma("sp", out=v_tail[(tt - 4) * 128:(tt - 3) * 128, t * 512:(t + 1) * 512], in_=vs, reads=[vk], writes=[("d_vt", t, tt)], ds=self.ds("vs%d" % (vi % 4)))
                vi += 1
            self.ring_done(1)
        self._coll("v_tail", [512, D], BF16, [("d_vt", t, tt) for t in range(4) for tt in range(4, 8)], "d_halo_v")
        for t in range(4):
            w, kw = self.ring_get()
            for c in range(4):
                h = 4 * t + c
                for th in range(2):
                    proj_fm(w, kw, c, th, QT[:, h, th * 512:(th + 1) * 512], "dve" if h % 2 == 0 else "act", [("QT", h)])
            self.ring_done(1)

    def ph_attn_b(self):
        P = self.P
        if not P.active:
            return
        P.barrier()
        HT, FL = self.HT, self.FL
        R0 = 2 * 16384
        QT = self.av(R0, [128, 16, TOK], BF16)
        o = R0 + 32768
        KH = [self.av(o + i * 3072, [128, 1536], BF16) for i in range(2)]
        VH = [self.av(o + 6144 + i * 3072, [128, 12, 128], BF16) for i in range(2)]
        BH = [self.av(o + 12288 + i * 2560, [128, 640], F32) for i in range(2)]
        kT_own = self.xt("kT_own", [D, TOK], BF16, False)
        v_own = self.xt("v_own", [TOK, D], BF16, False)
        kT_halo = self.halo_src("kT_tail", [D, 512], BF16)
        v_halo = self.halo_src("v_tail", [512, D], BF16)
        biasT = self.din("biasT", [16 * 128, 640])
        scale = 128.0 ** -0.5
        self._outproj_prefetch("att_w_o")
        TMPS = [self.av(o + 17408 + i * 2560, [128, 640], F32) for i in range(3)]
        PTS = [self.av(o + 25088 + i * 1280, [128, 640], BF16) for i in range(3)]
        RL = [self.av(o + 28928 + i * 512, [128, 128], F32) for i in range(2)]

        def load(h):
            s = h % 2
            hd = ("HD", s)
            dsx = self.ds("hd%d" % s)
            hs = slice(h * 128, (h + 1) * 128)
            rd = [("d_kTo", h), "d_halo_k", "d_halo_v"] + [("d_vo", h // 4, tt) for tt in range(8)]
            P.dma("sp", out=KH[s][:, 0:512], in_=kT_halo[hs, :], reads=rd, writes=[hd], ds=dsx)
            P.dma("sp", out=KH[s][:, 512:1536], in_=kT_own[hs, :], reads=rd, writes=[hd], ds=dsx)
            P.dma("sp", out=VH[s][:, 0:4, :], in_=v_halo[:, hs].rearrange("(j p) c -> p j c", p=128), reads=rd, writes=[hd], ds=dsx)
            P.dma("sp", out=VH[s][:, 4:12, :], in_=v_own[:, hs].rearrange("(j p) c -> p j c", p=128), reads=rd, writes=[hd], ds=dsx)
            P.dma("sp", out=BH[s], in_=biasT[hs, :], reads=rd, writes=[hd], ds=dsx)

        def stage1(idx, h, n):
            s = h % 2
            hd = ("HD", s)
            qsl = slice(n * 128, (n + 1) * 128)
            k = idx % 3
            bA = P.bank()
            bB = P.bank()
            pA, pB = self.psb(bA), self.psb(bB)
            mms = [(lambda e, i=i: e.matmul(pA[:, i * 128:(i + 1) * 128], lhsT=KH[s][:, (n + i) * 128:(n + i + 1) * 128], rhs=QT[:, h, qsl], start=True, stop=True))
                   for i in range(4)]
            mms.append(lambda e: e.matmul(pB[:, 0:128], lhsT=KH[s][:, (n + 4) * 128:(n + 5) * 128], rhs=QT[:, h, qsl], start=True, stop=True))
            P.mm_group(mms, reads=[hd, ("QT", h)], writes=[("PS", bA), ("PS", bB)])
            tmp = TMPS[k]
            tk = ("TMPS", k)
            P.op("dve", lambda e: e.scalar_tensor_tensor(out=tmp[:, 0:512], in0=pA, scalar=scale, in1=BH[s][:, 0:512], op0=ALU.mult, op1=ALU.add),
                 reads=[("PS", bA), hd], writes=[tk])
            P.op("dve", lambda e: e.scalar_tensor_tensor(out=tmp[:, 512:640], in0=pB[:, 0:128], scalar=scale, in1=BH[s][:, 512:640], op0=ALU.mult, op1=ALU.add),
                 reads=[("PS", bB), hd], writes=[tk])
            pt = PTS[k]
            pk = ("PTS", k)
            if n < 4:
                nh = (4 - n) * 128
                P.op("act", lambda e: e.activation(out=pt[:, 0:nh], in_=tmp[:, 0:nh], func=AF.Exp, bias=FL[:, 1:2]), reads=[tk, "FL"], writes=[pk])
                P.op("act", lambda e: e.activation(out=pt[:, nh:640], in_=tmp[:, nh:640], func=AF.Exp), reads=[tk], writes=[pk])
            else:
                P.op("act", lambda e: e.activation(out=pt, in_=tmp, func=AF.Exp), reads=[tk], writes=[pk])

        def stage2(idx, h, n):
            s = h % 2
            hd = ("HD", s)
            qsl = slice(n * 128, (n + 1) * 128)
            k = idx % 3
            pt = PTS[k]
            pk = ("PTS", k)
            bO = P.bank()
            pO = self.psb(bO)
            mm2 = [(lambda e, i=i: e.matmul(pO[:, 0:128], lhsT=VH[s][:, n + i, :], rhs=pt[:, i * 128:(i + 1) * 128], start=(i == 0), stop=(i == 4), skip_group_check=True))
                   for i in range(5)]
            mm2 += [(lambda e, i=i: e.matmul(pO[:, 128:256], lhsT=self.ONESB, rhs=pt[:, i * 128:(i + 1) * 128], start=False, stop=(i == 4), skip_group_check=True))
                    for i in range(5)]
            P.mm_group(mm2, reads=[pk, hd], writes=[("PS", bO)])
            rl = RL[idx % 2]
            rk = ("RL", idx % 2)
            P.op("dve", lambda e: e.reciprocal(out=rl, in_=pO[:, 128:256]), reads=[("PS", bO)], writes=[rk])
            P.op("dve", lambda e: e.tensor_tensor(out=HT[:, h, qsl], in0=pO[:, 0:128], in1=rl, op=ALU.mult), reads=[("PS", bO), rk], writes=[("CC", h, n)])

        items = [(h, n) for h in range(16) for n in range(8)]
        load(0)
        load(1)
        for idx, (h, n) in enumerate(items):
            stage1(idx, h, n)
            if idx >= 1:
                hp, np_ = items[idx - 1]
                stage2(idx - 1, hp, np_)
                if np_ == 7 and hp + 2 < 16:
                    load(hp + 2)
        stage2(len(items) - 1, *items[-1])
        P.barrier()
        self._outproj("att_w_o", prefetched=True)

    def _state_list(self):
        return [("st_xt", self.XT.rearrange("p a b -> p (a b)"), [128, KC * TOK], F32),
                ("st_ht", self.HT.rearrange("p a b -> p (a b)"), [128, KC * TOK], BF16),
                ("st_ar", self.AR, [128, self.ARB // 2], BF16),
                ("st_cf", self.CF, [128, 512], F32),
                ("st_vec", self.VEC, [128, NVROW], F32),
                ("st_ones", self.ONESB, [128, 128], BF16),
                ("st_identb", self.IDENTB, [128, 128], BF16),
                ("st_fl", self.FL, [128, 4], F32)]

    def dump_state(self):
        P = self.P
        P.wait_all("sp")
        for name, ap, shape, dt in self._state_list():
            t = self.dout(name + "_o", shape, dt)
            P.dma("sp", out=t, in_=ap, writes=[("dump", name)], ds=self.ds("dump_" + name))

    def restore_state(self):
        P = self.P
        for name, ap, shape, dt in self._state_list():
            t = self.din(name + "_i", shape, dt)
            P.dma("sp", out=ap, in_=t, writes=[("rest", name)], ds=self.ds("rest_" + name))
        P.barrier()

    def finish(self):
        P = self.P
        P.active = True
        P.wait_all("sp")
        P.emit()


def build_program(seg=None, upto=None, dbg_raw=False):
    B = Builder(seg)
    plan = [
        (0, lambda: B.ph_consts()),
        (0, lambda: B.ph_load_x()),
        (0, lambda: B.ph_ffn(0, R_FFN + 0)),
        (0, lambda: B.ph_mixer0_a(R_MIX + 0)),
        (1, lambda: B.ph_mixer0_b()),
        (1, lambda: B.ph_ffn(1, R_FFN + 16)),
        (1, lambda: B.ph_pl(0, R_PL + 0)),
        (1, lambda: B.ph_ffn(2, R_FFN + 32)),
        (1, lambda: B.ph_attn_a(R_MIX + 16)),
        (2, lambda: B.ph_attn_b()),
        (2, lambda: B.ph_ffn(3, R_FFN + 48)),
        (2, lambda: B.ph_pl(1, R_PL + 16)),
    ]
    if upto is not None:
        plan = plan[:upto]
    B.n_seg = plan[-1][0] + 1
    last_seg = 0
    for sg, fn in plan:
        if sg != last_seg:
            B.exchange(last_seg)
            last_seg = sg
        fn()
    if upto is not None:
        if dbg_raw:
            B.ph_dbg_ht()
        B.ph_final(normalize=False)
    else:
        B.ph_final(normalize=True)
    B.finish()
    return B


def _consts():
    c = np.zeros((128, 512), np.float32)
    c[:, 0:128] = np.eye(128, dtype=np.float32)
    s = np.arange(128)[:, None]
    t = np.arange(128)[None, :]
    m = ((s <= t) & ((s // 64) == (t // 64))).astype(np.float32)
    c[:, 128:256] = m
    c[:, 256:384] = m * np.float32(-1.0 / 16.0)
    c[:, 384:512] = 1.0
    return c


def _bias_index():
    sl = np.arange(128)[:, None, None]
    i = np.arange(5)[None, :, None]
    tl = np.arange(128)[None, None, :]
    delta = 4 - i
    rel = delta * 128 + tl - sl
    idx = np.clip(rel, -128, 128) + 128
    dc = 2 * delta + (tl >= 64).astype(np.int64) - (sl >= 64).astype(np.int64)
    valid = (dc >= 0) & (dc <= 8)
    return idx, valid


def host_inputs(inp):
    f = np.float32
    g = lambda k: np.asarray(inp[k], dtype=f)
    vec = np.zeros((NVROW, 128), f)
    vec[R_FFN:R_FFN + 64] = g("ffn_norm").reshape(64, 128)
    vec[R_MIX:R_MIX + 32] = g("mix_norm").reshape(32, 128)
    vec[R_PL:R_PL + 32] = g("pl_norm").reshape(32, 128)
    vec[R_FIN:R_FIN + 16] = g("final_norm").reshape(16, 128)
    vec[R_GLN:R_GLN + 2] = g("gla_norm_g").reshape(2, 128)
    vec[R_CB:R_CB + 8] = g("conv_dw_b").reshape(8, 128)
    vec[R_LNG:R_LNG + 8] = g("conv_ln_g").reshape(8, 128)
    vec[R_LNB:R_LNB + 8] = g("conv_ln_b").reshape(8, 128)
    vec[R_DW:R_DW + 248] = g("conv_dw").reshape(248, 128)
    gw = np.concatenate([g("gla_gate_w").reshape(16, 512), g("gla_gate_b").reshape(1, 512)], axis=0)
    idx, valid = _bias_index()
    rb = g("att_rel_bias").reshape(16, 257)
    biasT = np.where(valid[None], rb[:, idx], f(NEG)).astype(f).reshape(16 * 128, 640)
    shared = {
        "consts": _consts(), "vecs": vec, "gwaug": np.ascontiguousarray(gw), "biasT": np.ascontiguousarray(biasT),
        "ffn_w_gate": g("ffn_w_gate").reshape(4 * D, FF), "ffn_w_up": g("ffn_w_up").reshape(4 * D, FF),
        "ffn_w_down": g("ffn_w_down").reshape(4 * FF, D),
        "ab_w_in": g("ab_w_in").reshape(D, ABIN), "ab_w_out": g("ab_w_out").reshape(D, D),
        "att_w_qkv": g("att_w_qkv").reshape(D, 3 * D), "att_w_o": g("att_w_o").reshape(D, D),
        "pl_w_gate": g("pl_w_gate").reshape(2 * D, D), "pl_w_proj": g("pl_w_proj").reshape(2 * 256, D),
    }
    x = g("x")
    p = g("p")
    per = []
    for c in range(NCORES):
        b, h = c // 2, c % 2
        fl = np.zeros((128, 4), f)
        fl[:, 0] = float(h)
        fl[:, 1] = 0.0 if h == 1 else NEG
        dct = dict(shared)
        dct["x"] = np.ascontiguousarray(x[b, h * TOK:(h + 1) * TOK, :])
        dct["p"] = np.ascontiguousarray(p[:, b, h * TOK:(h + 1) * TOK, :]).reshape(2 * TOK, 256)
        dct["flags"] = fl
        per.append(dct)
    return per


def run_chain(inp, upto=None, ncore=NCORES, trace=False):
    host = host_inputs(inp)
    carry = [dict() for _ in range(ncore)]
    sg = 0
    times = []
    while True:
        B = build_program(seg=sg, upto=upto)
        in_maps = []
        for c in range(ncore):
            m = {}
            for n in B.in_names:
                m[n] = host[c][n] if n in host[c] else carry[c][n]
            in_maps.append(m)
        res = run_bass_kernel_spmd(B.nc, in_maps, core_ids=list(range(ncore)), trace=trace)
        times.append(res.exec_time_ns)
        if sg == B.n_seg - 1:
            return [res.results[c]["out"] for c in range(ncore)], times
        for c in range(ncore):
            carry[c] = {}
            for name, val in res.results[c].items():
                if name.endswith("_o"):
                    carry[c][name[:-2] + "_i"] = val
            first = res.results[2 * (c // 2)]
            for name, val in first.items():
                if name.endswith("_o"):
                    carry[c][name[:-2] + "_h"] = val
        sg += 1


def run_fused(inp, ncore=NCORES, trace=False):
    host = host_inputs(inp)
    B = build_program(seg=None)
    in_maps = [{n: host[c][n] for n in B.in_names} for c in range(ncore)]
    res = run_bass_kernel_spmd(B.nc, in_maps, core_ids=list(range(ncore)), trace=trace)
    return [res.results[c]["out"] for c in range(ncore)], [res.exec_time_ns]


def kernel(**inputs):
    if MODE == "fused":
        outs, _ = run_fused(inputs)
    else:
        outs, _ = run_chain(inputs)
    full = np.zeros((4, 2 * TOK, D), np.float32)
    for c in range(NCORES):
        full[c // 2, (c % 2) * TOK:(c % 2 + 1) * TOK, :] = outs[c]
    return full
```
